# Optimizing a Trainium2 kernel written in Bass

```python
import math
import jax, jax.numpy as jnp
from jax import lax
import numpy as np


D_MODEL = 1024
BATCH = 4
SEQ = 8192
DEPTH = 1
DEC_BATCH = 8
DEC_SEQ = 16
PAST_LEN = 1024

CHUNK = 64
N_META = 16
ATTN_WIDTH = 512
CONV_WIDTH = 512
MIX_WIDTH = ATTN_WIDTH + CONV_WIDTH
N_HEADS_A = 4
HEAD_DIM = ATTN_WIDTH // (2 * N_HEADS_A)
V_DIM = 2 * HEAD_DIM
N_CONV_GROUPS = 8
CONV_W = 3
ROT_DIM = HEAD_DIM // 4
ROPE_THETA = 500000.0
Q_BLOCK = 128
EPS = 1e-6
IN_COLS = 4 * ATTN_WIDTH + 4 * CONV_WIDTH
SPLITS = (ATTN_WIDTH, 2 * ATTN_WIDTH, 3 * ATTN_WIDTH, 4 * ATTN_WIDTH,
          4 * ATTN_WIDTH + CONV_WIDTH, 4 * ATTN_WIDTH + 2 * CONV_WIDTH,
          4 * ATTN_WIDTH + 3 * CONV_WIDTH)

kernel_name = 'hymba_diffattn_shortconv_stream_step'


def rms_norm(x, g):
    xf = x.astype(jnp.float32)
    y = xf * lax.rsqrt(jnp.mean(xf * xf, axis=-1, keepdims=True) + EPS)
    return (y * g.astype(jnp.float32)).astype(x.dtype)


def lambda_init(layer_idx):
    return 0.8 - 0.6 * math.exp(-0.3 * layer_idx)


def rope_partial(t, pos):
    half = ROT_DIM // 2
    inv = ROPE_THETA ** (-jnp.arange(0, ROT_DIM, 2, dtype=jnp.float32) / ROT_DIM)
    ang = pos.astype(jnp.float32)[:, None] * inv[None, :]
    cos = jnp.cos(ang)[:, None, None, :]
    sin = jnp.sin(ang)[:, None, None, :]
    tr = t[..., :ROT_DIM].astype(jnp.float32)
    x1, x2 = tr[..., :half], tr[..., half:]
    rot = jnp.concatenate([x1 * cos - x2 * sin, x2 * cos + x1 * sin], axis=-1)
    return jnp.concatenate([rot.astype(t.dtype), t[..., ROT_DIM:]], axis=-1)


def project(h, ln_g, w_in, qn_g, kn_g, pos):
    B, T, _ = h.shape
    p = rms_norm(h, ln_g) @ w_in
    q, k, v, ga, bb, cc, hh, gc = jnp.split(p, SPLITS, axis=-1)
    q = q.reshape(B, T, N_HEADS_A, 2, HEAD_DIM)
    k = k.reshape(B, T, N_HEADS_A, 2, HEAD_DIM)
    v = v.reshape(B, T, N_HEADS_A, V_DIM)
    q = rope_partial(rms_norm(q, qn_g), pos)
    k = rope_partial(rms_norm(k, kn_g), pos)
    u = cc * hh
    return q, k, v, ga, bb, u, gc


def diff_attn(q, k, v, mask, lam):
    s = jnp.einsum('bqhmd,bkhmd->bhmqk', q, k).astype(jnp.float32) * (HEAD_DIM ** -0.5)
    if mask is not None:
        s = jnp.where(mask, s, jnp.finfo(jnp.float32).min)
    p = jax.nn.softmax(s, axis=-1)
    a = p[:, :, 0] - lam * p[:, :, 1]
    return jnp.einsum('bhqk,bkhe->bqhe', a.astype(v.dtype), v)


def prompt_attention(q, k_all, v_all, lam):
    B, S = q.shape[0], q.shape[1]
    nb = S // Q_BLOCK
    key_chunk = jnp.concatenate([-jnp.ones((N_META,), jnp.int32),
                                 jnp.arange(S, dtype=jnp.int32) // CHUNK])
    qb = q.reshape(B, nb, Q_BLOCK, N_HEADS_A, 2, HEAD_DIM).transpose(1, 0, 2, 3, 4, 5)

    def one_block(args):
        q_blk, j = args
        q_chunk = (j * Q_BLOCK + jnp.arange(Q_BLOCK, dtype=jnp.int32)) // CHUNK
        mask = key_chunk[None, :] <= q_chunk[:, None]
        return diff_attn(q_blk, k_all, v_all, mask, lam)

    o = lax.map(one_block, (qb, jnp.arange(nb, dtype=jnp.int32)))
    return o.transpose(1, 0, 2, 3, 4).reshape(B, S, N_HEADS_A, V_DIM)


def causal_conv(u, prev, w, b):
    T = u.shape[1]
    up = jnp.concatenate([prev.astype(u.dtype), u], axis=1)
    y = b
    for j in range(CONV_W):
        y = y + w[j] * up[:, j:j + T]
    return y, up[:, -(CONV_W - 1):]


def finish(h, o_attn, sub_g, li, ga, conv_y, bb, gc, w_out):
    B, T = h.shape[0], h.shape[1]
    oa = (rms_norm(o_attn, sub_g) * (1.0 - li)).reshape(B, T, ATTN_WIDTH)
    oa = jax.nn.silu(ga) * oa
    oc = jax.nn.silu(gc) * (bb * conv_y)
    return h + jnp.concatenate([oa, oc], axis=-1) @ w_out


def setup_inputs(seed: int = 0) -> dict:
    key = jax.random.key(seed)
    ks = jax.random.split(key, 18)
    nrm = jax.random.normal
    f = jnp.float32
    return {
        'x_prompt': nrm(ks[0], (BATCH, SEQ, D_MODEL), f),
        'x_sample': nrm(ks[1], (DEC_BATCH, DEC_SEQ, D_MODEL), f),
        'cache_k': nrm(ks[2], (DEPTH, DEC_BATCH, PAST_LEN, N_HEADS_A, 2 * HEAD_DIM), f),
        'cache_v': nrm(ks[3], (DEPTH, DEC_BATCH, PAST_LEN, N_HEADS_A, V_DIM), f),
        'state_conv': nrm(ks[4], (DEPTH, DEC_BATCH, CONV_W - 1, CONV_WIDTH), f),
        'meta_tokens': nrm(ks[5], (N_META, D_MODEL), f),
        'ln_g': 1.0 + 0.01 * nrm(ks[6], (DEPTH, D_MODEL), f),
        'w_in': nrm(ks[7], (DEPTH, D_MODEL, IN_COLS), f) * D_MODEL ** -0.5,
        'q_norm_g': 1.0 + 0.01 * nrm(ks[8], (DEPTH, HEAD_DIM), f),
        'k_norm_g': 1.0 + 0.01 * nrm(ks[9], (DEPTH, HEAD_DIM), f),
        'lam_q1': 0.1 * nrm(ks[10], (DEPTH, HEAD_DIM), f),
        'lam_k1': 0.1 * nrm(ks[11], (DEPTH, HEAD_DIM), f),
        'lam_q2': 0.1 * nrm(ks[12], (DEPTH, HEAD_DIM), f),
        'lam_k2': 0.1 * nrm(ks[13], (DEPTH, HEAD_DIM), f),
        'subln_g': 1.0 + 0.01 * nrm(ks[14], (DEPTH, V_DIM), f),
        'conv_w': nrm(ks[15], (DEPTH, CONV_W, CONV_WIDTH), f) * CONV_W ** -0.5,
        'conv_b': 0.01 * nrm(ks[16], (DEPTH, CONV_WIDTH), f),
        'w_out': nrm(ks[17], (DEPTH, MIX_WIDTH, D_MODEL), f) * MIX_WIDTH ** -0.5,
    }


def reference(x_prompt, x_sample, cache_k, cache_v, state_conv, meta_tokens, ln_g, w_in,
              q_norm_g, k_norm_g, lam_q1, lam_k1, lam_q2, lam_k2, subln_g, conv_w,
              conv_b, w_out):
    B, S = x_prompt.shape[0], x_prompt.shape[1]
    DB, T = x_sample.shape[0], x_sample.shape[1]
    P = cache_k.shape[2]
    pos_m = jnp.arange(N_META, dtype=jnp.int32)
    pos_p = N_META + jnp.arange(S, dtype=jnp.int32)
    pos_s = N_META + P + jnp.arange(T, dtype=jnp.int32)

    hm = meta_tokens[None].astype(x_prompt.dtype)
    hp = x_prompt
    hs = x_sample
    kp_l, vp_l, cp_l, ks_l, vs_l, cs_l = [], [], [], [], [], []
    for l in range(DEPTH):
        li = lambda_init(l)
        lam = (jnp.exp(jnp.sum(lam_q1[l].astype(jnp.float32) * lam_k1[l].astype(jnp.float32)))
               - jnp.exp(jnp.sum(lam_q2[l].astype(jnp.float32) * lam_k2[l].astype(jnp.float32)))
               + li)
        qm, km, vm, gam, bm, um, gcm = project(hm, ln_g[l], w_in[l], q_norm_g[l], k_norm_g[l], pos_m)
        qp, kp, vp, gap, bp, up, gcp = project(hp, ln_g[l], w_in[l], q_norm_g[l], k_norm_g[l], pos_p)
        qs, ks, vs, gas, bs, us, gcs = project(hs, ln_g[l], w_in[l], q_norm_g[l], k_norm_g[l], pos_s)
        meta_tail = um[:, -(CONV_W - 1):]

        kp_all = jnp.concatenate([jnp.broadcast_to(km, (B,) + km.shape[1:]), kp], axis=1)
        vp_all = jnp.concatenate([jnp.broadcast_to(vm, (B,) + vm.shape[1:]), vp], axis=1)
        op = prompt_attention(qp, kp_all, vp_all, lam)
        yp, tail_p = causal_conv(up, jnp.broadcast_to(meta_tail, (B, CONV_W - 1, CONV_WIDTH)),
                                 conv_w[l], conv_b[l])

        ck = cache_k[l].reshape(DB, P, N_HEADS_A, 2, HEAD_DIM).astype(ks.dtype)
        ks_all = jnp.concatenate([jnp.broadcast_to(km, (DB,) + km.shape[1:]), ck, ks], axis=1)
        vs_all = jnp.concatenate([jnp.broadcast_to(vm, (DB,) + vm.shape[1:]),
                                  cache_v[l].astype(vs.dtype), vs], axis=1)
        os_ = diff_attn(qs, ks_all, vs_all, None, lam)
        ys, tail_s = causal_conv(us, state_conv[l], conv_w[l], conv_b[l])

        kp_l.append(kp_all.reshape(B, N_META + S, N_HEADS_A, 2 * HEAD_DIM))
        vp_l.append(vp_all)
        cp_l.append(tail_p)
        ks_l.append(ks.reshape(DB, T, N_HEADS_A, 2 * HEAD_DIM))
        vs_l.append(vs)
        cs_l.append(tail_s)

        if l < DEPTH - 1:
            om = diff_attn(qm, km, vm, None, lam)
            ym, _ = causal_conv(um, jnp.zeros((1, CONV_W - 1, CONV_WIDTH), um.dtype),
                                conv_w[l], conv_b[l])
            hm = finish(hm, om, subln_g[l], li, gam, ym, bm, gcm, w_out[l])
        hp = finish(hp, op, subln_g[l], li, gap, yp, bp, gcp, w_out[l])
        hs = finish(hs, os_, subln_g[l], li, gas, ys, bs, gcs, w_out[l])

    return (hp, hs, jnp.stack(kp_l), jnp.stack(vp_l), jnp.stack(cp_l),
            jnp.stack(ks_l), jnp.stack(vs_l), jnp.stack(cs_l))
```

```python
from contextlib import ExitStack
import numpy as np
import ml_dtypes
import concourse.bass as bass
import concourse.mybir as mybir
from concourse.bass_utils import run_bass_kernel_spmd

F32 = mybir.dt.float32
BF16 = mybir.dt.bfloat16
AF = mybir.ActivationFunctionType
ALU = mybir.AluOpType
AX = mybir.AxisListType

NQT = 16
S_TOK = 512 * NQT
N_OWN = 256 * NQT
NKP = 128 * max(4 * NQT + 1, 10)
NKS = 1280
EPS = 1e-6
LI = 0.2
NEG = -100.0


class Res:
    __slots__ = ("name", "w", "r", "dsem", "dcnt", "track")

    def __init__(self, name, track=True):
        self.name = name
        self.track = track
        self.w = None
        self.r = []
        self.dsem = None
        self.dcnt = 0


class _Rec:
    def __init__(self):
        self.call = None

    def __getattr__(self, name):
        def m(*a, **k):
            self.call = (name, a, k)
            return self
        return m


def _record(fn):
    r = _Rec()
    fn(r)
    name, a, k = r.call
    return lambda e: getattr(e, name)(*a, **k)


class Sched:
    ENG = ("pe", "act", "dve", "pool", "sp")

    def __init__(self, nc, stack):
        self.nc = nc
        self.stack = stack
        self.prog = {e: [] for e in self.ENG}
        self.sems = {}
        self.cnt = {}
        for e in ("pe", "act", "dve", "pool"):
            self.sems[e] = stack.enter_context(nc.semaphore("s_" + e))
            self.cnt[e] = 0
        self.waited = {e: {} for e in self.ENG}
        self.nres = 0
        self.dma_tokens = {}

    def res(self, name=None, track=True):
        self.nres += 1
        return Res("%s%d" % (name or "r", self.nres), track)

    def _need(self, eng, tok):
        if tok is None:
            return
        key, val = tok
        if self.waited[eng].get(key, 0) >= val:
            return
        self.waited[eng][key] = val
        sem = self.sems[key]
        self.prog[eng].append(lambda e, sem=sem, val=val: e.wait_ge(sem, val))

    def _deps(self, eng, reads, writes, skip_same=True):
        need = {}

        def add(t):
            if t is not None and need.get(t[0], 0) < t[1]:
                need[t[0]] = t[1]
        for r in reads:
            add(r.w)
        for w in writes:
            add(w.w)
            for t in w.r:
                if skip_same and t[0] == eng:
                    continue
                add(t)
        for k, v in need.items():
            self._need(eng, (k, v))

    def _commit(self, tok, reads, writes):
        for r in reads:
            r.r.append(tok)
            if len(r.r) > 24:
                best = {}
                for k, v in r.r:
                    if best.get(k, 0) < v:
                        best[k] = v
                r.r = list(best.items())
        for w in writes:
            w.w = tok
            w.r = []

    def op(self, eng, fn, reads=(), writes=()):
        fn = _record(fn)
        self._deps(eng, reads, writes)
        self.cnt[eng] += 1
        tok = (eng, self.cnt[eng])
        sem = self.sems[eng]
        self.prog[eng].append(lambda e, fn=fn, sem=sem: fn(e).then_inc(sem, 1))
        self._commit(tok, reads, writes)
        return tok

    def opn(self, eng, fn, reads=(), writes=()):
        fn = _record(fn)
        self._deps(eng, reads, writes)
        self.prog[eng].append(lambda e, fn=fn: fn(e))

    def dma(self, eng, fn, sres, reads=(), writes=()):
        fn = _record(fn)
        reads = [r for r in reads if r.track]
        writes = [w for w in writes if w.track]
        self._deps(eng, reads, writes, skip_same=False)
        if sres.dsem is None:
            key = "d%d_%s" % (len(self.sems), sres.name)
            sres.dsem = key
            self.sems[key] = self.stack.enter_context(self.nc.semaphore(key))
        sres.dcnt += 16
        tok = (sres.dsem, sres.dcnt)
        sem = self.sems[sres.dsem]
        self.prog[eng].append(lambda e, fn=fn, sem=sem: fn(e).then_inc(sem, 16))
        self.dma_tokens[sres.dsem] = sres.dcnt
        self._commit(tok, reads, writes)
        return tok

    def barrier(self):
        toks = [(k, v) for k, v in self.dma_tokens.items()]
        toks += [(e, self.cnt[e]) for e in ("pe", "act", "dve", "pool") if self.cnt[e]]
        for eng in self.ENG:
            for t in toks:
                if t[0] == eng:
                    continue
                self._need(eng, t)

    def emit(self):
        self.barrier()
        nc = self.nc
        with nc.Block() as block:
            def run(name):
                def body(e):
                    for f in self.prog[name]:
                        f(e)
                return body
            block.tensor(run("pe"))
            block.scalar(run("act"))
            block.vector(run("dve"))
            block.gpsimd(run("pool"))
            block.sync(run("sp"))


class Deferred:
    def __init__(self):
        self.q = []

    def add(self, n, fn):
        self.q.append([n, fn])

    def tick(self):
        for it in self.q:
            it[0] -= 1
        ready = [it for it in self.q if it[0] <= 0]
        self.q = [it for it in self.q if it[0] > 0]
        for it in ready:
            it[1]()

    def flush(self):
        while self.q:
            m = min(it[0] for it in self.q)
            for it in self.q:
                it[0] -= m
            self.tick() if m == 0 else self.tick_zero()

    def tick_zero(self):
        ready = [it for it in self.q if it[0] <= 0]
        self.q = [it for it in self.q if it[0] > 0]
        for it in ready:
            it[1]()


class Ring:
    def __init__(self, S, alloc, name, shape, dt, n):
        self.t = [alloc("%s%d" % (name, i), shape, dt) for i in range(n)]
        self.r = [S.res(name) for i in range(n)]
        self.i = -1
        self.n = n

    def next(self):
        self.i = (self.i + 1) % self.n
        return self.t[self.i], self.r[self.i]


class _Stop(Exception):
    pass


def build_program(nqt=16, stop=None):
    global NQT, S_TOK, N_OWN, NKP

    def stage(k):
        if stop is not None and stop == k:
            raise _Stop()

    NQT = nqt
    S_TOK = 512 * NQT
    N_OWN = 256 * NQT
    NKP = 128 * max(4 * NQT + 1, 10)
    nc = bass.Bass("TRN2", target_bir_lowering=False)
    di = lambda n, s, dt=F32: nc.dram_tensor(n, list(s), dt, kind="ExternalInput").ap()
    do = lambda n, s, dt=F32: nc.dram_tensor(n, list(s), dt, kind="ExternalOutput").ap()
    ds = lambda n, s, dt=BF16: nc.dram_tensor(n, list(s), dt, kind="Internal").ap()

    x_own = di("x_own", [N_OWN, 1024]); x_oth = di("x_oth", [N_OWN, 1024]); x_aux = di("x_aux", [128, 1024])
    cs_own = di("cs_own", [N_OWN, 32]); cs_oth = di("cs_oth", [N_OWN, 32]); cs_aux = di("cs_aux", [128, 32])
    cache_k = di("cache_k", [1024, 512]); cache_v = di("cache_v", [1024, 512]); state_conv = di("state_conv", [2, 512])
    w_in = di("w_in", [1024, 4096]); w_out = di("w_out", [1024, 1024]); ln_g = di("ln_g", [1024])
    qk_g = di("qk_g", [2, 64]); lam_in = di("lam_in", [4, 64]); subln_g = di("subln_g", [128])
    conv_w = di("conv_w", [3, 512]); conv_b = di("conv_b", [512])
    ident_d = di("ident", [128, 128], BF16); role_bias_d = di("role_bias", [128, 1])

    y_own = do("y_own", [N_OWN, 1024]); k_own = do("k_own", [N_OWN, 512]); v_own = do("v_own", [N_OWN, 512])
    k_aux = do("k_aux", [32, 512]); v_aux = do("v_aux", [32, 512]); y_s = do("y_s", [16, 1024])
    conv_p = do("conv_p", [2, 512]); conv_s = do("conv_s", [2, 512])

    kT_scr = ds("kT_scr", [128, 4, NKP]); v_scr = ds("v_scr", [NKP, 512])
    kTs_scr = ds("kTs_scr", [128, 4, NKS]); vs_scr = ds("vs_scr", [NKS, 512])
    NQC = N_OWN + 16
    qT_scr = ds("qT_scr", [128, 4, NQC]); sga_scr = ds("sga_scr", [128, 4, NQC]); mixc_scr = ds("mixc_scr", [128, 4, NQC])

    with ExitStack() as top:
        S = Sched(nc, top)
        r_kTs = S.res("kTs", False); r_vs = S.res("vs", False)
        r_qscr = [S.res("qscr", False) for _ in range(NQT + 1)]
        r_gscr = [S.res("gscr", False) for _ in range(NQT + 1)]
        r_mscr = [S.res("mscr", False) for _ in range(NQT + 1)]
        r_kblk = [S.res("kblk", False) for _ in range(NQT + 1)]

        try:
          with ExitStack() as p1:
              sb = lambda n, s, dt: p1.enter_context(nc.sbuf_tensor("a_" + n, list(s), dt))
              psum = lambda n, s, dt: p1.enter_context(nc.psum_tensor("a_" + n, list(s), dt))
              ring = lambda n, s, dt, k: Ring(S, sb, n, s, dt, k)

              wbf = sb("wbf", [128, 8, 4096], BF16); r_wbfA = S.res("wbfA"); r_wbfB = S.res("wbfB")
              ident = sb("ident", [128, 128], BF16); r_ident = S.res("ident")
              gt = sb("gt", [128, 8], F32); r_gt = S.res("gt")
              gqk = sb("gqk", [128, 2, 8, 64], F32); r_gqk = S.res("gqk")
              cw = sb("cw", [128, 4, 3], F32); cb = sb("cb", [128, 4], F32); r_cw = S.res("cw")
              mhalf = sb("mhalf", [128, 8], F32); r_mhalf = S.res("mhalf")
              uprev = sb("uprev", [128, 4, 32], F32); r_uprev = S.res("uprev")
              stT = sb("stT", [128, 4, 2], F32); r_stT = S.res("stT")
              utail = sb("utail", [128, 4, 2], F32); r_utail = S.res("utail")
              utail_s = sb("utail_s", [128, 4, 2], F32); r_utail_s = S.res("utail_s")

              xt_ring = ring("xt", [128, 1024], F32, 4)
              xs_ring = ring("xs", [128, 1024], F32, 2)
              xstat = [sb("xstat%d" % i, [128, 8], F32) for i in range(3)]; r_xstat = [S.res("xstat") for i in range(3)]
              junk = sb("junk", [128, 1024], BF16); r_junk = S.res("junk")
              st_ring = ring("stat", [128, 8], F32, 12)
              xn_ring = ring("xn", [128, 1024], BF16, 3)
              xnT_ring = ring("xnT", [128, 8, 512], BF16, 2)
              cs_ring = ring("cs", [128, 32], F32, 6)
              sq_ring = ring("sq", [128, 512], F32, 3)
              tq_ring = ring("tq", [128, 512], F32, 6)
              rt_ring = ring("rt", [128, 4, 8, 8], F32, 4)
              tb_ring = ring("tb", [128, 512], BF16, 4)
              vf_ring = ring("vf", [128, 512], F32, 2)
              vb_ring = ring("vb", [128, 512], BF16, 3)
              kst_ring = ring("kst", [128, 4, 512], BF16, 2)
              qst_ring = ring("qst", [128, 4, 512], BF16, 2)
              csb_ring = ring("csb", [128, 512], F32, 2)
              uext_ring = ring("uext", [128, 2, 258], F32, 2)
              ysb_ring = ring("ysb", [128, 512], F32, 2)
              sgc_ring = ring("sgc", [128, 512], F32, 2)
              gst_ring = ring("gst", [128, 4, 512], BF16, 2)
              mst_ring = ring("mst", [128, 4, 512], BF16, 2)

              pT = psum("pT", [128, 8, 128], BF16); r_pT = S.res("pT")
              pTq = psum("pTq", [128, 8, 128], BF16); r_pTq = S.res("pTq")
              pA = Ring(S, psum, "pA", [128, 512], F32, 6)

              S.dma("sp", lambda e: e.dma_start(out=gt[:], in_=ln_g.rearrange("(kt p) -> p kt", p=128),
                                                allow_slow_non_contiguous=True), r_gt, writes=[r_gt])
              S.dma("sp", lambda e: e.dma_start(out=ident[:], in_=ident_d), r_ident, writes=[r_ident])
              S.op("pool", lambda e: e.memset(mhalf[:], -0.5), writes=[r_mhalf])

              def col(v1d):
                  return v1d.rearrange("(p o) -> p o", o=1)

              def late_consts():
                  for a in range(2):
                      S.dma("sp", lambda e, a=a: e.dma_start(out=gqk[:, a, :, :],
                                                             in_=qk_g[a, :].partition_broadcast(128).unsqueeze(1).to_broadcast([128, 8, 64])),
                            r_gqk, writes=[r_gqk] if a == 1 else [])
                  for c in range(4):
                      for j in range(3):
                          S.dma("sp", lambda e: e.dma_start(out=cw[:, c, j:j + 1], in_=col(conv_w[j, c * 128:(c + 1) * 128])), r_cw)
                      S.dma("sp", lambda e: e.dma_start(out=cb[:, c:c + 1], in_=col(conv_b[c * 128:(c + 1) * 128])), r_cw,
                            writes=[r_cw] if c == 3 else [])
                      for t in range(2):
                          S.dma("sp", lambda e: e.dma_start(out=stT[:, c, t:t + 1], in_=col(state_conv[t, c * 128:(c + 1) * 128])), r_stT,
                                writes=[r_stT] if (c == 3 and t == 1) else [])

              def weight_chunks(kt):
                  for hh in range(4):
                      wt, rw = xt_ring.next()
                      S.dma("sp", lambda e: e.dma_start(out=wt[:], in_=w_in[kt * 128:(kt + 1) * 128, hh * 1024:(hh + 1) * 1024]), rw, writes=[rw])
                      if hh < 2:
                          S.op("dve", lambda e: e.tensor_scalar(out=wbf[:, kt, hh * 1024:(hh + 1) * 1024], in0=wt[:], scalar1=gt[:, kt:kt + 1],
                                                                scalar2=None, op0=ALU.mult), reads=[rw, r_gt], writes=[r_wbfA])
                      else:
                          S.op("act", lambda e: e.activation(out=wbf[:, kt, hh * 1024:(hh + 1) * 1024], in_=wt[:], func=AF.Copy,
                                                             scale=gt[:, kt:kt + 1]), reads=[rw, r_gt], writes=[r_wbfB])

              stage(1)
              def rsqrt_small(src_ap, n, mean_div, reads, out_res_pair):
                  o, ro = out_res_pair
                  S.op("dve", lambda e: e.tensor_scalar(out=o[:, 0:n], in0=src_ap, scalar1=1.0 / mean_div, scalar2=EPS,
                                                        op0=ALU.mult, op1=ALU.add), reads=reads, writes=[ro])
                  S.op("pool", lambda e: e.tensor_tensor(out=o[:, 0:n], in0=o[:, 0:n], in1=mhalf[:, 0:n], op=ALU.pow),
                       reads=[ro, r_mhalf], writes=[ro])
                  return o, ro

              def xload(src_rows):
                  xt, rx = xt_ring.next()
                  S.dma("sp", lambda e: e.dma_start(out=xt[:], in_=src_rows), rx, writes=[rx])
                  return xt, rx

              def xprep(src_rows, xnT, r_xnT, col0, loaded=None, rstd=None):
                  xt, rx = loaded if loaded is not None else xload(src_rows)
                  if rstd is None:
                      stt, rs_ = st_ring.next()
                      S.op("act", lambda e: e.activation(out=junk[:], in_=xt[:], func=AF.Square, accum_out=stt[:, 7:8]),
                           reads=[rx], writes=[r_junk, rs_])
                      S.op("dve", lambda e: e.tensor_scalar(out=stt[:, 0:1], in0=stt[:, 7:8], scalar1=1.0 / 1024, scalar2=EPS,
                                                            op0=ALU.mult, op1=ALU.add), reads=[rs_], writes=[rs_])
                      S.op("pool", lambda e: e.tensor_tensor(out=stt[:, 0:1], in0=stt[:, 0:1], in1=mhalf[:, 0:1], op=ALU.pow),
                           reads=[rs_, r_mhalf], writes=[rs_])
                      rstd = (stt[:, 0:1], rs_)
                  rstd_ap, rs_ = rstd
                  xn, rxn = xn_ring.next()
                  S.op("act", lambda e: e.activation(out=xn[:], in_=xt[:], func=AF.Copy, scale=rstd_ap),
                       reads=[rx, rs_], writes=[rxn])
                  for kt in range(8):
                      f = lambda e, kt=kt: e.transpose(out=pT[:, kt, :], in_=xn[:, kt * 128:(kt + 1) * 128], identity=ident[:])
                      if kt < 7:
                          S.opn("pe", f, reads=[rxn, r_ident], writes=[r_pT])
                      else:
                          S.op("pe", f, reads=[rxn, r_ident], writes=[r_pT])
                  S.op("dve", lambda e: e.tensor_copy(out=xnT[:, :, col0:col0 + 128], in_=pT[:]), reads=[r_pT], writes=[r_xnT])

              def proj_tok(xnT, r_xnT, col0, wcol0, bank=None):
                  pp, rp = pA.next()
                  for kt in range(8):
                      f = lambda e, kt=kt: e.matmul(pp[:], lhsT=xnT[:, kt, col0:col0 + 128], rhs=wbf[:, kt, wcol0:wcol0 + 512],
                                                    start=(kt == 0), stop=(kt == 7))
                      if kt < 7:
                          S.opn("pe", f, reads=[r_xnT, r_wbfA, r_wbfB], writes=[rp])
                      else:
                          S.op("pe", f, reads=[r_xnT, r_wbfA, r_wbfB], writes=[rp])
                  return pp, rp

              def qk_norm_rope(pp, rp, which, cs, rcs):
                  sq, rsq = sq_ring.next()
                  S.op("act", lambda e: e.activation(out=sq[:], in_=pp[:], func=AF.Square), reads=[rp], writes=[rsq])
                  stt, rs_ = st_ring.next()
                  S.op("dve", lambda e: e.tensor_reduce(out=stt[:, 0:8], in_=sq[:].rearrange("p (a b) -> p a b", b=64),
                                                        axis=AX.X, op=ALU.add), reads=[rsq], writes=[rs_])
                  S.op("dve", lambda e: e.tensor_scalar(out=stt[:, 0:8], in0=stt[:, 0:8], scalar1=1.0 / 64, scalar2=EPS,
                                                        op0=ALU.mult, op1=ALU.add), reads=[rs_], writes=[rs_])
                  S.op("pool", lambda e: e.tensor_tensor(out=stt[:, 0:8], in0=stt[:, 0:8], in1=mhalf[:, 0:8], op=ALU.pow),
                       reads=[rs_, r_mhalf], writes=[rs_])
                  t, rt_ = tq_ring.next()
                  t3 = t[:].rearrange("p (a b) -> p a b", b=64)
                  S.op("dve", lambda e: e.tensor_tensor(out=t3, in0=pp[:].rearrange("p (a b) -> p a b", b=64),
                                                        in1=stt[:, 0:8].unsqueeze(2).to_broadcast([128, 8, 64]), op=ALU.mult),
                       reads=[rp, rs_], writes=[rt_])
                  S.op("dve", lambda e: e.tensor_tensor(out=t3, in0=t3, in1=gqk[:, which, :, :], op=ALU.mult),
                       reads=[rt_, r_gqk], writes=[rt_])
                  tmp, rtm = rt_ring.next()
                  tmp2 = tmp[:].rearrange("p a b c -> p (a b c)").rearrange("p (a g d) -> p a g d", a=2, g=8)
                  ccb = cs[:, 0:16].unsqueeze(1).to_broadcast([128, 8, 16])
                  nsb = cs[:, 16:24].unsqueeze(1).to_broadcast([128, 8, 8])
                  psb = cs[:, 24:32].unsqueeze(1).to_broadcast([128, 8, 8])
                  x1 = t3[:, :, 0:8]; x2 = t3[:, :, 8:16]
                  S.op("pool", lambda e: e.tensor_tensor(out=tmp2[:, 0, :, :], in0=t3[:, :, 0:16], in1=ccb, op=ALU.mult), reads=[rt_, rcs], writes=[rtm])
                  S.op("pool", lambda e: e.tensor_tensor(out=tmp2[:, 1, :, 0:8], in0=x2, in1=nsb, op=ALU.mult), reads=[rt_, rcs], writes=[rtm])
                  S.op("pool", lambda e: e.tensor_tensor(out=tmp2[:, 1, :, 8:16], in0=x1, in1=psb, op=ALU.mult), reads=[rt_, rcs], writes=[rtm])
                  S.op("pool", lambda e: e.tensor_tensor(out=t3[:, :, 0:16], in0=tmp2[:, 0, :, :], in1=tmp2[:, 1, :, :], op=ALU.add), reads=[rtm], writes=[rt_])
                  return t, rt_

              def to_featmajor(t, rt_, half, dst, rdst, col0, ncols=128):
                  tb, rtb = tb_ring.next()
                  S.op("act", lambda e: e.activation(out=tb[:], in_=t[:], func=AF.Copy), reads=[rt_], writes=[rtb])
                  rq = r_pTq
                  for h in range(4):
                      f = lambda e, h=h: e.transpose(out=pTq[:, half * 4 + h, :], in_=tb[:, h * 128:(h + 1) * 128], identity=ident[:])
                      if h < 3:
                          S.opn("pe", f, reads=[rtb, r_ident], writes=[rq])
                      else:
                          S.op("pe", f, reads=[rtb, r_ident], writes=[rq])
                  S.op("dve", lambda e: e.tensor_copy(out=dst[:, :, col0:col0 + ncols], in_=pTq[:, half * 4:half * 4 + 4, 0:ncols]),
                       reads=[rq], writes=[rdst])

              def proj_feat(xnT, r_xnT, n, wcol0):
                  pf, rpf = pA.next()
                  for kt in range(8):
                      f = lambda e, kt=kt: e.matmul(pf[:, 0:n], lhsT=wbf[:, kt, wcol0:wcol0 + 128], rhs=xnT[:, kt, 0:n],
                                                    start=(kt == 0), stop=(kt == 7))
                      if kt < 7:
                          S.opn("pe", f, reads=[r_xnT, r_wbfA, r_wbfB], writes=[rpf])
                      else:
                          S.op("pe", f, reads=[r_xnT, r_wbfA, r_wbfB], writes=[rpf])
                  return pf, rpf

              def conv_chain(c, ue, rue, nseg, L, pB, rpB, sg, rsg, out_ap, rout):
                  ys, rys = ysb_ring.next()
                  y3 = ys[:, 0:nseg * L].rearrange("p (s l) -> p s l", l=L)
                  S.op("dve", lambda e: e.tensor_scalar(out=y3, in0=ue[:, 0:nseg, 0:L], scalar1=cw[:, c, 0:1], scalar2=cb[:, c:c + 1],
                                                        op0=ALU.mult, op1=ALU.add), reads=[rue, r_cw], writes=[rys])
                  S.op("dve", lambda e: e.scalar_tensor_tensor(out=y3, in0=ue[:, 0:nseg, 1:L + 1], scalar=cw[:, c, 1:2], in1=y3,
                                                               op0=ALU.mult, op1=ALU.add), reads=[rue, r_cw, rys], writes=[rys])
                  S.op("dve", lambda e: e.scalar_tensor_tensor(out=y3, in0=ue[:, 0:nseg, 2:L + 2], scalar=cw[:, c, 2:3], in1=y3,
                                                               op0=ALU.mult, op1=ALU.add), reads=[rue, r_cw, rys], writes=[rys])
                  S.op("dve", lambda e: e.tensor_tensor(out=y3, in0=pB, in1=y3, op=ALU.mult), reads=[rpB, rys], writes=[rys])
                  S.op("pool", lambda e: e.tensor_tensor(out=out_ap, in0=y3, in1=sg, op=ALU.mult), reads=[rys, rsg], writes=[rout])

              cst = {}

              def cache_tile(i):
                  ck, rck = tq_ring.next()
                  S.dma("sp", lambda e: e.dma_start(out=ck[:], in_=cache_k[i * 128:(i + 1) * 128, :]), rck, writes=[rck])
                  if i % 4 == 0:
                      cst["kst"] = kst_ring.next()
                  kst_c, rkst_c = cst["kst"]
                  to_featmajor(ck, rck, i % 2, kst_c, rkst_c, (i % 4) * 128)
                  if i % 4 == 3:
                      p0 = 128 + (i - 3) * 128
                      S.dma("sp", lambda e: e.dma_start(out=kTs_scr[:, :, p0:p0 + 512], in_=kst_c[:]), rkst_c, reads=[rkst_c], writes=[r_kTs])
                  cv, rcv = vf_ring.next()
                  S.dma("sp", lambda e: e.dma_start(out=cv[:], in_=cache_v[i * 128:(i + 1) * 128, :]), rcv, writes=[rcv])
                  vb, rvb = vb_ring.next()
                  S.op("dve", lambda e: e.tensor_copy(out=vb[:], in_=cv[:]), reads=[rcv], writes=[rvb])
                  S.dma("sp", lambda e: e.dma_start(out=vs_scr[128 + i * 128:128 + (i + 1) * 128, :], in_=vb[:]), rvb, reads=[rvb], writes=[r_vs])

              for kt in range(8):
                  weight_chunks(kt)
                  if kt == 0:
                      late_consts()
                  cache_tile(kt)

              xnTa, r_xnTa = xnT_ring.next()
              xprep(x_aux, xnTa, r_xnTa, 0)
              csa, rcsa = cs_ring.next()
              S.dma("sp", lambda e: e.dma_start(out=csa[:], in_=cs_aux), rcsa, writes=[rcsa])
              kst_a, rkst_a = kst_ring.next()
              qst_a, rqst_a = qst_ring.next()
              pq, rpq = proj_tok(xnTa, r_xnTa, 0, 0, 0)
              pk, rpk = proj_tok(xnTa, r_xnTa, 0, 512, 1)
              pv, rpv = proj_tok(xnTa, r_xnTa, 0, 1024, 2)
              tqa, rtqa = qk_norm_rope(pq, rpq, 0, csa, rcsa)
              tka, rtka = qk_norm_rope(pk, rpk, 1, csa, rcsa)
              S.dma("sp", lambda e: e.dma_start(out=k_aux, in_=tka[0:32, :]), rtka, reads=[rtka])
              vfa, rvfa = vf_ring.next()
              S.op("act", lambda e: e.activation(out=vfa[:], in_=pv[:], func=AF.Copy), reads=[rpv], writes=[rvfa])
              S.dma("sp", lambda e: e.dma_start(out=v_aux, in_=vfa[0:32, :]), rvfa, reads=[rvfa])
              vba, rvba = vb_ring.next()
              S.op("dve", lambda e: e.tensor_copy(out=vba[:], in_=vfa[:]), reads=[rvfa], writes=[rvba])
              S.dma("sp", lambda e: e.dma_start(out=v_scr[0:16, :], in_=vba[0:16, :]), rvba, reads=[rvba], writes=[r_kblk[NQT]])
              S.dma("sp", lambda e: e.dma_start(out=vs_scr[0:16, :], in_=vba[0:16, :]), rvba, reads=[rvba], writes=[r_vs])
              S.dma("sp", lambda e: e.dma_start(out=vs_scr[1152:1168, :], in_=vba[16:32, :]), rvba, reads=[rvba], writes=[r_vs])
              to_featmajor(tka, rtka, 0, kst_a, rkst_a, 0)
              to_featmajor(tqa, rtqa, 1, qst_a, rqst_a, 0)
              S.dma("sp", lambda e: e.dma_start(out=kT_scr[:, :, 0:16], in_=kst_a[:, :, 0:16]), rkst_a, reads=[rkst_a], writes=[r_kblk[NQT]])
              S.dma("sp", lambda e: e.dma_start(out=kTs_scr[:, :, 0:16], in_=kst_a[:, :, 0:16]), rkst_a, reads=[rkst_a], writes=[r_kTs])
              S.dma("sp", lambda e: e.dma_start(out=kTs_scr[:, :, 1152:1168], in_=kst_a[:, :, 16:32]), rkst_a, reads=[rkst_a], writes=[r_kTs])
              S.dma("sp", lambda e: e.dma_start(out=qT_scr[:, :, N_OWN:N_OWN + 16], in_=qst_a[:, :, 16:32]), rqst_a, reads=[rqst_a], writes=[r_qscr[NQT]])
              stage(2)
              gsa, rgsa = gst_ring.next()
              for h in range(4):
                  pf, rpf = proj_feat(xnTa, r_xnTa, 128, 1536 + h * 128)
                  S.op("act", lambda e, pf=pf, h=h: e.activation(out=gsa[:, h, 0:128], in_=pf[:, 0:128], func=AF.Silu), reads=[rpf], writes=[rgsa])
              S.dma("sp", lambda e: e.dma_start(out=sga_scr[:, :, N_OWN:N_OWN + 16], in_=gsa[:, :, 16:32]), rgsa, reads=[rgsa], writes=[r_gscr[NQT]])
              msa, rmsa = mst_ring.next()
              for c in range(4):
                  pf, rpf = proj_feat(xnTa, r_xnTa, 128, 2560 + c * 128)
                  csb, rcsb = csb_ring.next()
                  S.op("act", lambda e, pf=pf, csb=csb: e.activation(out=csb[:, 0:128], in_=pf[:, 0:128], func=AF.Copy), reads=[rpf], writes=[rcsb])
                  pf, rpf = proj_feat(xnTa, r_xnTa, 128, 3072 + c * 128)
                  ue, rue = uext_ring.next()
                  ua = sq_ring.next()
                  S.op("dve", lambda e, pf=pf, csb=csb, ua=ua: e.tensor_tensor(out=ua[0][:, 0:128], in0=pf[:, 0:128], in1=csb[:, 0:128], op=ALU.mult),
                       reads=[rpf, rcsb], writes=[ua[1]])
                  S.op("pool", lambda e, ua=ua, c=c: e.tensor_copy(out=uprev[:, c, :], in_=ua[0][:, 32:64]), reads=[ua[1]], writes=[r_uprev])
                  S.op("pool", lambda e, ua=ua, ue=ue: e.tensor_copy(out=ue[:, 0, 2:18], in_=ua[0][:, 16:32]), reads=[ua[1]], writes=[rue])
                  S.op("pool", lambda e, ue=ue, c=c: e.tensor_copy(out=ue[:, 0, 0:2], in_=stT[:, c, :]), reads=[r_stT], writes=[rue])
                  S.op("pool", lambda e, ue=ue, c=c: e.tensor_copy(out=utail_s[:, c, :], in_=ue[:, 0, 16:18]), reads=[rue], writes=[r_utail_s])
                  pfB, rpfB = proj_feat(xnTa, r_xnTa, 128, 2048 + c * 128)
                  pfG, rpfG = proj_feat(xnTa, r_xnTa, 128, 3584 + c * 128)
                  sg, rsg = sgc_ring.next()
                  S.op("act", lambda e, pfG=pfG, sg=sg: e.activation(out=sg[:, 0:16], in_=pfG[:, 16:32], func=AF.Silu), reads=[rpfG], writes=[rsg])
                  conv_chain(c, ue, rue, 1, 16, pfB[:, 16:32].unsqueeze(1), rpfB, sg[:, 0:16].unsqueeze(1), rsg,
                             msa[:, c, 0:16].unsqueeze(1), rmsa)
              S.dma("sp", lambda e: e.dma_start(out=mixc_scr[:, :, N_OWN:N_OWN + 16], in_=msa[:, :, 0:16]), rmsa, reads=[rmsa], writes=[r_mscr[NQT]])
              for c in range(4):
                  for t in range(2):
                      S.dma("sp", lambda e: e.dma_start(out=col(conv_s[t, c * 128:(c + 1) * 128]), in_=utail_s[:, c, t:t + 1]),
                            r_utail_s, reads=[r_utail_s])

              stage(3)
              stage(4)
              tiles = [(J, own, t) for J in range(NQT // 2) for own in (True, False) for t in range(4)]
              T = [dict() for _ in tiles]
              sup = {}

              def st_xload(n):
                  J, own, t = tiles[n]
                  row0 = J * 512 + t * 128
                  xsrc = x_own if own else x_oth
                  cssrc = cs_own if own else cs_oth
                  T[n]["xl"] = xload(xsrc[row0:row0 + 128, :])
                  cs, rcs = cs_ring.next()
                  S.dma("sp", lambda e: e.dma_start(out=cs[:], in_=cssrc[row0:row0 + 128, :]), rcs, writes=[rcs])
                  T[n].update(cs=cs, rcs=rcs, row0=row0)

              supers = [(J, own) for J in range(NQT // 2) for own in (True, False)]

              stats_pending = {}

              def st_stats_load(si, t):
                  J, own = supers[si]
                  row0 = J * 512 + t * 128
                  xsrc = x_own if own else x_oth
                  xs, rxs = xs_ring.next()
                  S.dma("act", lambda e: e.dma_start(out=xs[:], in_=xsrc[row0:row0 + 128, :]), rxs, writes=[rxs])
                  stats_pending[(si, t)] = (xs, rxs)

              def st_stats_sq(si, t):
                  xs, rxs = stats_pending.pop((si, t))
                  st_ = xstat[si % 3]
                  S.op("act", lambda e: e.activation(out=junk[:], in_=xs[:], func=AF.Square, accum_out=st_[:, 4 + t:5 + t]),
                       reads=[rxs], writes=[r_junk, r_xstat[si % 3]])

              def st_stats(si, t):
                  st_stats_load(si, t)
                  st_stats_sq(si, t)

              def st_stats_finish(si):
                  st_ = xstat[si % 3]
                  rr = r_xstat[si % 3]
                  S.op("dve", lambda e: e.tensor_scalar(out=st_[:, 0:4], in0=st_[:, 4:8], scalar1=1.0 / 1024, scalar2=EPS,
                                                        op0=ALU.mult, op1=ALU.add), reads=[rr], writes=[rr])
                  S.op("pool", lambda e: e.tensor_tensor(out=st_[:, 0:4], in0=st_[:, 0:4], in1=mhalf[:, 0:4], op=ALU.pow),
                       reads=[rr, r_mhalf], writes=[rr])

              def st_xprep(n):
                  J, own, t = tiles[n]
                  if t == 0:
                      sup[(J, own)] = dict(xnT=xnT_ring.next())
                  xnT, r_xnT = sup[(J, own)]["xnT"]
                  si = supers.index((J, own))
                  xprep(None, xnT, r_xnT, t * 128, loaded=T[n]["xl"], rstd=(xstat[si % 3][:, t:t + 1], r_xstat[si % 3]))

              def st_proj(n):
                  J, own, t = tiles[n]
                  xnT, r_xnT = sup[(J, own)]["xnT"]
                  T[n]["pk"] = proj_tok(xnT, r_xnT, t * 128, 512)
                  T[n]["pv"] = proj_tok(xnT, r_xnT, t * 128, 1024)
                  if own:
                      T[n]["pq"] = proj_tok(xnT, r_xnT, t * 128, 0)

              def st_chain(n):
                  J, own, t = tiles[n]
                  d = T[n]
                  cs, rcs, row0 = d["cs"], d["rcs"], d["row0"]
                  pk, rpk = d["pk"]
                  pv, rpv = d["pv"]
                  tk, rtk = qk_norm_rope(pk, rpk, 1, cs, rcs)
                  if own:
                      S.dma("sp", lambda e: e.dma_start(out=k_own[row0:row0 + 128, :], in_=tk[:]), rtk, reads=[rtk])
                  vf, rvf = vf_ring.next()
                  S.op("act", lambda e: e.activation(out=vf[:], in_=pv[:], func=AF.Copy), reads=[rpv], writes=[rvf])
                  if own:
                      S.dma("sp", lambda e: e.dma_start(out=v_own[row0:row0 + 128, :], in_=vf[:]), rvf, reads=[rvf])
                  vb, rvb = vb_ring.next()
                  S.op("dve", lambda e: e.tensor_copy(out=vb[:], in_=vf[:]), reads=[rvf], writes=[rvb])
                  j = 2 * J + t // 2
                  pos = 128 + 512 * j + (0 if own else 256) + (t % 2) * 128
                  S.dma("sp", lambda e: e.dma_start(out=v_scr[pos:pos + 128, :], in_=vb[:]), rvb, reads=[rvb], writes=[r_kblk[j]])
                  d.update(tk=tk, rtk=rtk)
                  if own:
                      pq, rpq = d["pq"]
                      tq, rtq = qk_norm_rope(pq, rpq, 0, cs, rcs)
                      d.update(tq=tq, rtq=rtq)

              def st_featT(n):
                  J, own, t = tiles[n]
                  d = T[n]
                  su = sup[(J, own)]
                  if t == 0:
                      su["kst"] = kst_ring.next()
                      if own:
                          su["qst"] = qst_ring.next()
                  kst, rkst = su["kst"]
                  to_featmajor(d["tk"], d["rtk"], 0, kst, rkst, t * 128)
                  if own:
                      qst, rqst = su["qst"]
                      to_featmajor(d["tq"], d["rtq"], 1, qst, rqst, t * 128)
                  if t == 3:
                      for s_ in range(2):
                          j = 2 * J + s_
                          pos = 128 + 512 * j + (0 if own else 256)
                          S.dma("sp", lambda e: e.dma_start(out=kT_scr[:, :, pos:pos + 256], in_=kst[:, :, s_ * 256:(s_ + 1) * 256]),
                                rkst, reads=[rkst], writes=[r_kblk[j]])
                      if own:
                          S.dma("sp", lambda e: e.dma_start(out=qT_scr[:, :, J * 512:(J + 1) * 512], in_=qst[:]), rqst,
                                reads=[rqst], writes=[r_qscr[2 * J], r_qscr[2 * J + 1]])
                  T[n].clear()

              def st_fm(J):
                  xnT, r_xnT = sup[(J, True)]["xnT"]
                  gs, rgs = gst_ring.next()
                  for h in range(4):
                      pf, rpf = proj_feat(xnT, r_xnT, 512, 1536 + h * 128)
                      S.op("act", lambda e: e.activation(out=gs[:, h, :], in_=pf[:], func=AF.Silu), reads=[rpf], writes=[rgs])
                  S.dma("sp", lambda e: e.dma_start(out=sga_scr[:, :, J * 512:(J + 1) * 512], in_=gs[:]), rgs,
                        reads=[rgs], writes=[r_gscr[2 * J], r_gscr[2 * J + 1]])
                  ms, rms = mst_ring.next()
                  for c in range(4):
                      pfC, rpfC = proj_feat(xnT, r_xnT, 512, 2560 + c * 128)
                      csb, rcsb = csb_ring.next()
                      S.op("act", lambda e: e.activation(out=csb[:], in_=pfC[:], func=AF.Copy), reads=[rpfC], writes=[rcsb])
                      pfH, rpfH = proj_feat(xnT, r_xnT, 512, 3072 + c * 128)
                      ue, rue = uext_ring.next()
                      S.op("dve", lambda e: e.tensor_tensor(
                          out=ue[:, :, 2:258], in0=pfH[:].rearrange("p (s l) -> p s l", l=256),
                          in1=csb[:].rearrange("p (s l) -> p s l", l=256), op=ALU.mult), reads=[rpfH, rcsb], writes=[rue])
                      for s_ in range(2):
                          j = 2 * J + s_
                          S.op("pool", lambda e: e.tensor_copy(out=ue[:, s_, 0:2], in_=uprev[:, c, 2 * j:2 * j + 2]),
                               reads=[r_uprev], writes=[rue])
                      if J == NQT // 2 - 1:
                          S.op("pool", lambda e: e.tensor_copy(out=utail[:, c, :], in_=ue[:, 1, 256:258]), reads=[rue], writes=[r_utail])
                      pfB, rpfB = proj_feat(xnT, r_xnT, 512, 2048 + c * 128)
                      pfG, rpfG = proj_feat(xnT, r_xnT, 512, 3584 + c * 128)
                      sg, rsg = sgc_ring.next()
                      S.op("act", lambda e: e.activation(out=sg[:], in_=pfG[:], func=AF.Silu), reads=[rpfG], writes=[rsg])
                      conv_chain(c, ue, rue, 2, 256, pfB[:].rearrange("p (s l) -> p s l", l=256), rpfB,
                                 sg[:].rearrange("p (s l) -> p s l", l=256), rsg,
                                 ms[:, c, :].rearrange("p (s l) -> p s l", l=256), rms)
                  S.dma("sp", lambda e: e.dma_start(out=mixc_scr[:, :, J * 512:(J + 1) * 512], in_=ms[:]), rms,
                        reads=[rms], writes=[r_mscr[2 * J], r_mscr[2 * J + 1]])

              NT = len(tiles)
              for si0 in range(min(2, len(supers))):
                  for t_ in range(4):
                      st_stats(si0, t_)
                  st_stats_finish(si0)
              st_xload(0)
              st_xload(1)
              st_xload(2)
              st_xprep(0)
              st_xprep(1)
              for n in range(NT):
                  J_, own_, t_ = tiles[n]
                  si_ = supers.index((J_, own_))
                  do_stats = si_ + 2 < len(supers) and t_ < 2
                  if do_stats:
                      st_stats_load(si_ + 2, 2 * t_)
                      st_stats_load(si_ + 2, 2 * t_ + 1)
                  if n + 3 < NT:
                      st_xload(n + 3)
                  if n + 2 < NT:
                      st_xprep(n + 2)
                  st_proj(n)
                  if n >= 2:
                      st_featT(n - 2)
                  st_chain(n)
                  if do_stats:
                      st_stats_sq(si_ + 2, 2 * t_)
                      st_stats_sq(si_ + 2, 2 * t_ + 1)
                      if t_ == 1:
                          st_stats_finish(si_ + 2)
                  J, own, t = tiles[n]
                  if own and t == 3:
                      st_fm(J)
              st_featT(NT - 2)
              st_featT(NT - 1)
              for c in range(4):
                  for t in range(2):
                      S.dma("sp", lambda e: e.dma_start(out=col(conv_p[t, c * 128:(c + 1) * 128]), in_=utail[:, c, t:t + 1]),
                            r_utail, reads=[r_utail])
              S.barrier()

          stage(5)
          with ExitStack() as p2:
              sb = lambda n, s, dt: p2.enter_context(nc.sbuf_tensor("b_" + n, list(s), dt))
              psum = lambda n, s, dt: p2.enter_context(nc.psum_tensor("b_" + n, list(s), dt))
              ring = lambda n, s, dt, k: Ring(S, sb, n, s, dt, k)

              KT = sb("KT", [128, 4, NKP], BF16)
              Vs = sb("Vs", [128, NKP // 128, 512], BF16)
              r_kt = [S.res("ktK") for _ in range(NKP // 128)]
              r_ktv = [S.res("ktV") for _ in range(NKP // 128)]
              wobf = sb("wobf", [128, 8, 1024], BF16); r_wobf = S.res("wobf")
              ones = sb("ones", [128, 128], BF16); r_ones = S.res("ones")
              rbias = sb("rbias", [128, 1], F32); r_rbias = S.res("rbias")
              lamt = sb("lamt", [128, 4, 64], F32); r_lamt = S.res("lamt")
              lamp = sb("lamp", [128, 2, 64], F32); r_lamp = S.res("lamp")
              lams = sb("lams", [128, 4], F32); r_lams = S.res("lams")
              gsub = sb("gsub", [128, 1], F32); r_gsub = S.res("gsub")
              wo_ring = ring("wost", [128, 256], F32, 2)
              qt_ring = ring("QT", [128, 4, 2, 256], BF16, 2)
              ones16 = sb("ones16", [128, 128], BF16); r_ones16 = S.res("ones16")
              sga_ring = ring("SGA", [128, 4, 256], BF16, 2)
              mix_ring = ring("MIX", [128, 8, 256], BF16, 2)
              e_ring = ring("E", [128, 512], BF16, 5)
              rsb_ring = ring("Rsb", [128, 512], F32, 2)
              tsb_ring = ring("Tsb", [128, 512], F32, 1)
              osb_ring = ring("Osb", [128, 256], F32, 3)
              o2_ring = ring("O2", [128, 256], F32, 1)
              sqb_ring = ring("sqb", [128, 256], BF16, 3)
              vv_ring = ring("vv", [128, 256], F32, 3)
              xres_ring = ring("xres", [128, 512], F32, 2)
              yo_ring = ring("yo", [128, 512], F32, 2)
              acc_ring = ring("acc", [128, 512], F32, 2)
              hl_ring = ring("hl", [128, 2, 512], BF16, 2)

              pS = Ring(S, psum, "pS", [128, 512], F32, 3)
              pO = Ring(S, psum, "pO", [128, 512], F32, 2)
              pD = Ring(S, psum, "pD", [128, 512], F32, 2)
              pY = psum("pY", [128, 512], F32); r_pY = S.res("pY")
              pSS, r_pSS = pY, r_pY

              S.op("pool", lambda e: e.memset(ones[:], 1.0), writes=[r_ones])
              S.op("pool", lambda e: e.memset(ones16[:], 0.0), writes=[r_ones16])
              S.op("pool", lambda e: e.memset(ones16[0:16, :], 1.0), writes=[r_ones16])
              for _i in range(2):
                  S.op("pool", lambda e: e.memset(qt_ring.t[_i][:], 0.0), writes=[qt_ring.r[_i]])
              S.dma("sp", lambda e: e.dma_start(out=rbias[:], in_=role_bias_d), r_rbias, writes=[r_rbias])
              S.dma("sp", lambda e: e.dma_start(out=lamt[:].rearrange("p a b -> p (a b)"),
                                                in_=lam_in.rearrange("a b -> (a b)").partition_broadcast(128)), r_lamt, writes=[r_lamt])
              S.dma("sp", lambda e: e.dma_start(out=gsub[:], in_=subln_g.rearrange("(p o) -> p o", o=1)), r_gsub, writes=[r_gsub])
              S.op("dve", lambda e: e.tensor_scalar(out=gsub[:], in0=gsub[:], scalar1=1.0 - LI, scalar2=None, op0=ALU.mult),
                   reads=[r_gsub], writes=[r_gsub])
              S.op("dve", lambda e: e.tensor_tensor(out=lamp[:], in0=lamt[:, 0:4:2, :], in1=lamt[:, 1:4:2, :], op=ALU.mult),
                   reads=[r_lamt], writes=[r_lamp])
              S.op("dve", lambda e: e.tensor_reduce(out=lams[:, 0:2], in_=lamp[:], axis=AX.X, op=ALU.add), reads=[r_lamp], writes=[r_lams])
              S.op("act", lambda e: e.activation(out=lams[:, 0:2], in_=lams[:, 0:2], func=AF.Exp), reads=[r_lams], writes=[r_lams])
              S.op("dve", lambda e: e.tensor_tensor(out=lams[:, 2:3], in0=lams[:, 1:2], in1=lams[:, 0:1], op=ALU.subtract),
                   reads=[r_lams], writes=[r_lams])
              S.op("dve", lambda e: e.tensor_scalar(out=lams[:, 3:4], in0=lams[:, 2:3], scalar1=-LI, scalar2=None, op0=ALU.add),
                   reads=[r_lams], writes=[r_lams])
              neglam = lams[:, 3:4]
              def wout_chunk(c, qq):
                  wt, rw = wo_ring.next()
                  S.dma("sp", lambda e: e.dma_start(out=wt[:], in_=w_out[c * 128:(c + 1) * 128, qq * 256:(qq + 1) * 256]), rw, writes=[rw])
                  S.op("dve", lambda e: e.tensor_copy(out=wobf[:, c, qq * 256:(qq + 1) * 256], in_=wt[:]), reads=[rw], writes=[r_wobf])

              stage(6)
              def load_keys(src_kT, src_v, rsrc, p0, npos, tiles):
                  if npos == 16:
                      t0 = tiles[0]
                      S.op("pool", lambda e: e.memset(KT[:, :, p0:p0 + 128], 0.0), writes=[r_kt[t0]])
                      S.op("pool", lambda e: e.memset(Vs[:, t0, :], 0.0), writes=[r_ktv[t0]])
                  S.dma("sp", lambda e: e.dma_start(out=KT[:, :, p0:p0 + npos], in_=src_kT[:, :, p0:p0 + npos]), r_kt[tiles[0]],
                        reads=rsrc, writes=[r_kt[t] for t in tiles])
                  if npos == 16:
                      S.dma("sp", lambda e: e.dma_start(out=Vs[0:16, t0, :], in_=src_v[p0:p0 + 16, :]), r_ktv[t0], reads=rsrc, writes=[r_ktv[t0]])
                  else:
                      nt = len(tiles)
                      S.dma("sp", lambda e: e.dma_start(out=Vs[:, tiles[0]:tiles[0] + nt, :],
                                                        in_=src_v[p0:p0 + npos, :].rearrange("(t p) c -> p t c", p=128)),
                            r_ktv[tiles[0]], reads=rsrc, writes=[r_ktv[t] for t in tiles])

              def load_qtile(j, nq, xsrc_rows):
                  c0 = j * 256
                  qt, rqt = qt_ring.next()
                  S.dma("sp", lambda e: e.dma_start(out=qt[0:64, :, 0, 0:nq], in_=qT_scr[0:64, :, c0:c0 + nq]), rqt, reads=[r_qscr[j]])
                  S.dma("sp", lambda e: e.dma_start(out=qt[64:128, :, 1, 0:nq], in_=qT_scr[64:128, :, c0:c0 + nq]), rqt, reads=[r_qscr[j]], writes=[rqt])
                  sg, rsg = sga_ring.next()
                  S.dma("sp", lambda e: e.dma_start(out=sg[:, :, 0:nq], in_=sga_scr[:, :, c0:c0 + nq]), rsg, reads=[r_gscr[j]], writes=[rsg])
                  mx, rmx = mix_ring.next()
                  if nq < 128:
                      S.op("pool", lambda e: e.memset(mx[:], 0.0), writes=[rmx])
                  S.dma("sp", lambda e: e.dma_start(out=mx[:, 4:8, 0:nq], in_=mixc_scr[:, :, c0:c0 + nq]), rmx, reads=[r_mscr[j]], writes=[rmx])
                  return (qt, rqt, sg, rsg, mx, rmx)

              defer = Deferred()

              def attention(nq, bufs, key_tiles, next_nkt=99):
                  qt, rqt, sg, rsg, mx, rmx = bufs
                  n2 = 2 * nq
                  for h in range(4):
                      O, rO = pO.next()
                      acc, racc = acc_ring.next()
                      nkt = len(key_tiles)
                      Sbufs = [None] * nkt

                      def qk(i):
                          kt, nk, mask = key_tiles[i]
                          kp = 128 * kt
                          Sb, rS = pS.next()
                          Sbufs[i] = (Sb, rS)
                          S.op("pe", lambda e: e.matmul(Sb[:, 0:n2].rearrange("p (c q) -> p c q", c=2), lhsT=KT[:, h, kp:kp + 128],
                                                        rhs=qt[:, h, :, 0:nq], start=True, stop=True), reads=[r_kt[kt], rqt], writes=[rS])

                      qk(0)
                      if nkt > 1:
                          qk(1)
                      D, rD = pD.next()
                      dst = {"started": False}
                      for i in range(nkt):
                          kt, nk, mask = key_tiles[i]
                          Sb, rS = Sbufs[i]
                          E, rE = e_ring.next()
                          if mask == "role":
                              S.op("act", lambda e: e.activation(out=E[:, 0:n2], in_=Sb[:, 0:n2], func=AF.Exp, scale=0.125,
                                                                 bias=rbias[:, :]), reads=[rS, r_rbias], writes=[rE])
                          else:
                              S.op("act", lambda e: e.activation(out=E[:, 0:n2], in_=Sb[:, 0:n2], func=AF.Exp, scale=0.125),
                                   reads=[rS], writes=[rE])
                          E3 = E[:, 0:n2].rearrange("p (c q) -> p c q", c=2)
                          S3 = Sb[:, 0:n2].rearrange("p (c q) -> p c q", c=2)
                          if mask == "diagA":
                              S.op("act", lambda e: e.activation(out=E3[64:128, :, 0:64], in_=S3[64:128, :, 0:64], func=AF.Copy, scale=0.0),
                                   reads=[rS], writes=[rE])
                          elif mask == "diagB":
                              S.op("act", lambda e: e.activation(out=E3[:, :, 0:128], in_=S3[:, :, 0:128], func=AF.Copy, scale=0.0),
                                   reads=[rS], writes=[rE])
                              S.op("act", lambda e: e.activation(out=E3[64:128, :, 128:192], in_=S3[64:128, :, 128:192], func=AF.Copy, scale=0.0),
                                   reads=[rS], writes=[rE])
                          if i + 2 < nkt:
                              qk(i + 2)
                          S.op("pe", lambda e: e.matmul(O[:, 0:n2], lhsT=Vs[:, kt, h * 128:(h + 1) * 128], rhs=E[:, 0:n2],
                                                        start=(i == 0), stop=(i == nkt - 1)), reads=[r_ktv[kt], rE], writes=[rO])
                          if i == 0:
                              S.op("dve", lambda e: e.memset(acc[:, 0:n2], 0.0), writes=[racc])
                          if i % 2 == 0 or nk != 128:
                              S.op("dve", lambda e: e.tensor_tensor(out=acc[0:nk, 0:n2], in0=acc[0:nk, 0:n2], in1=E[0:nk, 0:n2], op=ALU.add),
                                   reads=[rE, racc], writes=[racc])
                          else:
                              S.op("pe", lambda e: e.matmul(D[:, 0:n2], lhsT=ones[:, :], rhs=E[:, 0:n2], start=(not dst["started"]), stop=False),
                                   reads=[r_ones, rE], writes=[rD])
                              dst["started"] = True
                          defer.tick()
                      finish_head(h, nq, O, rO, acc, racc, sg, rsg, mx, rmx, D, rD, dst["started"], min(nkt, nkt if h < 3 else next_nkt))

              def finish_head(h, nq, O, rO, acc, racc, sg, rsg, mx, rmx, D, rD, d_started, nkt_head):
                  n2 = 2 * nq
                  st = {}

                  def f_a0():
                      hl, rhl = hl_ring.next()
                      S.op("dve", lambda e: e.tensor_copy(out=hl[:, 0, 0:n2], in_=acc[:, 0:n2]), reads=[racc], writes=[rhl])
                      S.op("dve", lambda e: e.tensor_tensor(out=hl[:, 1, 0:n2], in0=acc[:, 0:n2], in1=hl[:, 0, 0:n2], op=ALU.subtract),
                           reads=[racc, rhl], writes=[rhl])
                      st.update(hl=hl, rhl=rhl)

                  def f_a1():
                      hl, rhl = st["hl"], st["rhl"]
                      S.opn("pe", lambda e: e.matmul(D[:, 0:n2], lhsT=ones[:, :], rhs=hl[:, 0, 0:n2], start=(not d_started), stop=False),
                            reads=[r_ones, rhl], writes=[rD])
                      S.op("pe", lambda e: e.matmul(D[:, 0:n2], lhsT=ones[:, :], rhs=hl[:, 1, 0:n2], start=False, stop=True),
                           reads=[r_ones, rhl], writes=[rD])

                  def f_a2():
                      Rs, rRs = rsb_ring.next()
                      S.op("act", lambda e: e.activation(out=Rs[:, 0:n2], in_=D[:, 0:n2], func=AF.Ln), reads=[rD], writes=[rRs])
                      S.op("act", lambda e: e.activation(out=Rs[:, 0:n2], in_=Rs[:, 0:n2], func=AF.Exp, scale=-1.0), reads=[rRs], writes=[rRs])
                      st.update(Rs=Rs, rRs=rRs)

                  def f_a3():
                      Rs, rRs = st["Rs"], st["rRs"]
                      Ts, rTs = tsb_ring.next()
                      S.op("dve", lambda e: e.tensor_tensor(out=Ts[:, 0:n2], in0=O[:, 0:n2], in1=Rs[:, 0:n2], op=ALU.mult),
                           reads=[rO, rRs], writes=[rTs])
                      Os, rOs = osb_ring.next()
                      S.op("dve", lambda e: e.scalar_tensor_tensor(out=Os[:, 0:nq], in0=Ts[:, nq:n2], scalar=neglam, in1=Ts[:, 0:nq],
                                                                   op0=ALU.mult, op1=ALU.add), reads=[rTs, r_lams], writes=[rOs])
                      st.update(Os=Os, rOs=rOs)

                  def f_a4():
                      Os, rOs = st["Os"], st["rOs"]
                      sqb, rsqb = sqb_ring.next()
                      S.op("act", lambda e: e.activation(out=sqb[:, 0:nq], in_=Os[:, 0:nq], func=AF.Square), reads=[rOs], writes=[rsqb])
                      st.update(sqb=sqb, rsqb=rsqb)

                  def f_b1():
                      sqb, rsqb = st["sqb"], st["rsqb"]
                      S.op("pe", lambda e: e.matmul(pSS[:, 0:nq], lhsT=ones[:], rhs=sqb[:, 0:nq], start=True, stop=True),
                           reads=[r_ones, rsqb], writes=[r_pSS])

                  def f_b2():
                      vv, rvv = vv_ring.next()
                      S.op("dve", lambda e: e.tensor_scalar(out=vv[:, 0:nq], in0=pSS[:, 0:nq], scalar1=1.0 / 128, scalar2=EPS,
                                                            op0=ALU.mult, op1=ALU.add), reads=[r_pSS], writes=[rvv])
                      st.update(vv=vv, rvv=rvv)

                  def f_b3():
                      vv, rvv = st["vv"], st["rvv"]
                      S.op("act", lambda e: e.activation(out=vv[:, 0:nq], in_=vv[:, 0:nq], func=AF.Ln), reads=[rvv], writes=[rvv])
                      S.op("act", lambda e: e.activation(out=vv[:, 0:nq], in_=vv[:, 0:nq], func=AF.Exp, scale=-0.5), reads=[rvv], writes=[rvv])

                  def f_c():
                      Os, rOs, vv, rvv = st["Os"], st["rOs"], st["vv"], st["rvv"]
                      O2, rO2 = o2_ring.next()
                      S.op("dve", lambda e: e.tensor_tensor(out=O2[:, 0:nq], in0=Os[:, 0:nq], in1=vv[:, 0:nq], op=ALU.mult),
                           reads=[rOs, rvv], writes=[rO2])
                      S.op("dve", lambda e: e.scalar_tensor_tensor(out=mx[:, h, 0:nq], in0=O2[:, 0:nq], scalar=gsub[:, 0:1], in1=sg[:, h, 0:nq],
                                                                   op0=ALU.mult, op1=ALU.mult), reads=[rO2, r_gsub, rsg], writes=[rmx])

                  def f_b12():
                      f_b1()
                      f_b2()

                  if nkt_head >= 9:
                      sched = ((1, f_a0), (3, f_a1), (5, f_a2), (7, f_a3), (9, f_a4), (11, f_b12), (14, f_b3), (16, f_c))
                  else:
                      sched = ((1, f_a0), (2, f_a1), (3, f_a2), (4, f_a3), (5, f_a4), (7, f_b12), (9, f_b3), (11, f_c))
                  for dly, fn in sched:
                      defer.add(dly, fn)

              OPW = 256

              def out_proj_q(nq, bufs, xsrc, ydst, th, ch):
                  qt, rqt, sg, rsg, mx, rmx = bufs
                  nt = min(128, nq)
                  mt = 128
                  xr, rxr = xres_ring.next()
                  S.dma("sp", lambda e: e.dma_start(out=xr[0:nt, 0:OPW], in_=xsrc[th * 128:th * 128 + nt, ch * OPW:(ch + 1) * OPW]),
                        rxr, writes=[rxr])
                  yo, ryo = yo_ring.next()
                  for c in range(8):
                      f = lambda e, c=c: e.matmul(pY[0:mt, 0:OPW], lhsT=mx[:, c, th * 128:th * 128 + mt],
                                                  rhs=wobf[:, c, ch * OPW:(ch + 1) * OPW], start=(c == 0), stop=(c == 7))
                      if c < 7:
                          S.opn("pe", f, reads=[rmx, r_wobf], writes=[r_pY])
                      else:
                          S.op("pe", f, reads=[rmx, r_wobf], writes=[r_pY])
                  S.op("dve", lambda e: e.tensor_tensor(out=yo[0:nt, 0:OPW], in0=pY[0:nt, 0:OPW], in1=xr[0:nt, 0:OPW], op=ALU.add),
                       reads=[r_pY, rxr], writes=[ryo])
                  S.dma("sp", lambda e: e.dma_start(out=ydst[th * 128:th * 128 + nt, ch * OPW:(ch + 1) * OPW], in_=yo[0:nt, 0:OPW]),
                        ryo, reads=[ryo])

              def out_proj_deferred(nq, bufs, xsrc, ydst, first):
                  k = 0
                  for th in range((nq + 127) // 128):
                      for ch in range(1024 // OPW):
                          defer.add(first + 2 * k, lambda th=th, ch=ch: out_proj_q(nq, bufs, xsrc, ydst, th, ch))
                          k += 1
                  return first + 2 * k

              load_keys(kT_scr, v_scr, [r_kblk[NQT]], 0, 16, [0])
              load_keys(kTs_scr, vs_scr, [r_kTs, r_vs], 128, 512, [1, 2, 3, 4])
              load_keys(kTs_scr, vs_scr, [r_kTs, r_vs], 640, 512, [5, 6, 7, 8])
              load_keys(kTs_scr, vs_scr, [r_kTs, r_vs], 1152, 16, [9])
              bufs_s = load_qtile(NQT, 16, None)
              for _k in range(32):
                  defer.add(_k + 1, lambda _k=_k: wout_chunk(_k // 4, _k % 4))
              attention(16, bufs_s, [(0, 16, None)] + [(t, 128, None) for t in range(1, 9)] + [(9, 16, None)], next_nkt=5)
              out_proj_deferred(16, bufs_s, x_aux[16:32, :], y_s, 18)

              stage(7)
              def prefetch(j):
                  b = load_qtile(j, 256, None)
                  load_keys(kT_scr, v_scr, [r_kblk[j]], 128 + 512 * j, 512, [4 * j + 1 + i for i in range(4)])
                  return b

              pending = {0: prefetch(0)}
              for j in range(NQT):
                  bufs = pending.pop(j)
                  kts = [(0, 16, None)] + [(t, 128, None) for t in range(1, 4 * j + 1)]
                  kts += [(4 * j + 1, 128, "diagA"), (4 * j + 2, 128, "diagB"), (4 * j + 3, 128, "role"), (4 * j + 4, 128, "role")]
                  if j + 1 < NQT:
                      defer.add(35, lambda j=j: pending.__setitem__(j + 1, prefetch(j + 1)))
                  attention(256, bufs, kts)
                  out_proj_deferred(256, bufs, x_own[j * 256:(j + 1) * 256, :], y_own[j * 256:(j + 1) * 256, :], 18)
                  if j + 1 < NQT and (j + 1) not in pending:
                      defer.flush()
              defer.flush()
              S.barrier()
        except _Stop:
            pass
        S.emit()
    return nc


_NC_CACHE = {}


def _rope_tables(pos):
    inv = (500000.0 ** (-np.arange(0, 16, 2, dtype=np.float32) / 16)).astype(np.float32)
    ang = pos.astype(np.float32)[:, None] * inv[None, :]
    return np.concatenate([np.cos(ang), np.cos(ang), -np.sin(ang), np.sin(ang)], axis=1).astype(np.float32)


def make_in_maps(x_prompt, x_sample, cache_k, cache_v, state_conv, meta_tokens, ln_g, w_in, q_norm_g, k_norm_g,
                 lam_q1, lam_k1, lam_q2, lam_k2, subln_g, conv_w, conv_b, w_out, cores=range(8)):
    f = lambda a: np.ascontiguousarray(np.asarray(a, dtype=np.float32))
    x_prompt = f(x_prompt); x_sample = f(x_sample); cache_k = f(cache_k); cache_v = f(cache_v)
    state_conv = f(state_conv); meta_tokens = f(meta_tokens)
    ident = np.eye(128, dtype=np.float32).astype(ml_dtypes.bfloat16)
    qk_g = np.stack([f(q_norm_g)[0], f(k_norm_g)[0]])
    lam_in = np.stack([f(lam_q1)[0], f(lam_k1)[0], f(lam_q2)[0], f(lam_k2)[0]])
    in_maps = []
    tok_idx = {}
    for c in cores:
        b, r = c // 2, c % 2
        own = (np.arange(NQT)[:, None] * 512 + 256 * r + np.arange(256)[None, :]).reshape(-1)
        oth = (np.arange(NQT)[:, None] * 512 + 256 * (1 - r) + np.arange(256)[None, :]).reshape(-1)
        tok_idx[c] = own
        x_aux = np.zeros((128, 1024), np.float32)
        x_aux[0:16] = meta_tokens
        x_aux[16:32] = x_sample[c]
        pos_aux = np.zeros(128, np.float32)
        pos_aux[0:16] = np.arange(16)
        pos_aux[16:32] = 16 + 1024 + np.arange(16)
        for j in range(NQT):
            t0 = 512 * j + 256 * r
            for d in range(2):
                t = t0 - 2 + d
                x_aux[32 + 2 * j + d] = meta_tokens[16 + t] if t < 0 else x_prompt[b, t]
        in_maps.append(dict(
            x_own=np.ascontiguousarray(x_prompt[b, own]), x_oth=np.ascontiguousarray(x_prompt[b, oth]), x_aux=x_aux,
            cs_own=_rope_tables(16 + own), cs_oth=_rope_tables(16 + oth), cs_aux=_rope_tables(pos_aux),
            cache_k=np.ascontiguousarray(cache_k[0, c].reshape(1024, 512)),
            cache_v=np.ascontiguousarray(cache_v[0, c].reshape(1024, 512)),
            state_conv=np.ascontiguousarray(state_conv[0, c]),
            w_in=f(w_in)[0], w_out=f(w_out)[0], ln_g=f(ln_g)[0], qk_g=qk_g, lam_in=lam_in, subln_g=f(subln_g)[0],
            conv_w=f(conv_w)[0], conv_b=f(conv_b)[0], ident=ident,
            role_bias=np.full((128, 1), 0.0 if r == 1 else NEG, np.float32),
        ))
    return in_maps, tok_idx


def kernel(**inputs):
    B = 4
    if "nc" not in _NC_CACHE:
        _NC_CACHE["nc"] = build_program(16)
    nc = _NC_CACHE["nc"]
    in_maps, tok_idx = make_in_maps(**inputs)
    res = run_bass_kernel_spmd(nc, in_maps, core_ids=list(range(8)))
    R = res.results

    y_prompt = np.zeros((B, S_TOK, 1024), np.float32)
    k_prompt = np.zeros((1, B, 16 + S_TOK, 4, 128), np.float32)
    v_prompt = np.zeros((1, B, 16 + S_TOK, 4, 128), np.float32)
    conv_prompt = np.zeros((1, B, 2, 512), np.float32)
    y_sample = np.zeros((8, 16, 1024), np.float32)
    k_sample = np.zeros((1, 8, 16, 4, 128), np.float32)
    v_sample = np.zeros((1, 8, 16, 4, 128), np.float32)
    conv_sample = np.zeros((1, 8, 2, 512), np.float32)
    for c in range(8):
        b, r = c // 2, c % 2
        own = tok_idx[c]
        y_prompt[b, own] = R[c]["y_own"]
        k_prompt[0, b, 16 + own] = R[c]["k_own"].reshape(N_OWN, 4, 128)
        v_prompt[0, b, 16 + own] = R[c]["v_own"].reshape(N_OWN, 4, 128)
        if r == 0:
            k_prompt[0, b, 0:16] = R[c]["k_aux"][0:16].reshape(16, 4, 128)
            v_prompt[0, b, 0:16] = R[c]["v_aux"][0:16].reshape(16, 4, 128)
        else:
            conv_prompt[0, b] = R[c]["conv_p"]
        y_sample[c] = R[c]["y_s"]
        k_sample[0, c] = R[c]["k_aux"][16:32].reshape(16, 4, 128)
        v_sample[0, c] = R[c]["v_aux"][16:32].reshape(16, 4, 128)
        conv_sample[0, c] = R[c]["conv_s"]
    return (y_prompt, y_sample, k_prompt, v_prompt, conv_prompt, k_sample, v_sample, conv_sample)
```

```python
from contextlib import ExitStack
import numpy as np
import ml_dtypes
import concourse.bass as bass
import concourse.mybir as mybir
from concourse.bass_utils import run_bass_kernel_spmd

F32 = mybir.dt.float32
BF16 = mybir.dt.bfloat16
AF = mybir.ActivationFunctionType
ALU = mybir.AluOpType
AX = mybir.AxisListType

NQT = 16
S_TOK = 512 * NQT
N_OWN = 256 * NQT
NKP = 128 * max(4 * NQT + 1, 10)
NKS = 1280
EPS = 1e-6
LI = 0.2
NEG = -100.0


class Res:
    __slots__ = ("name", "w", "r", "dsem", "dcnt", "track")

    def __init__(self, name, track=True):
        self.name = name
        self.track = track
        self.w = None
        self.r = []
        self.dsem = None
        self.dcnt = 0


class _Rec:
    def __init__(self):
        self.call = None

    def __getattr__(self, name):
        def m(*a, **k):
            self.call = (name, a, k)
            return self
        return m


def _record(fn):
    r = _Rec()
    fn(r)
    name, a, k = r.call
    return lambda e: getattr(e, name)(*a, **k)


class Sched:
    ENG = ("pe", "act", "dve", "pool", "sp")

    def __init__(self, nc, stack):
        self.nc = nc
        self.stack = stack
        self.prog = {e: [] for e in self.ENG}
        self.sems = {}
        self.cnt = {}
        for e in ("pe", "act", "dve", "pool"):
            self.sems[e] = stack.enter_context(nc.semaphore("s_" + e))
            self.cnt[e] = 0
        self.waited = {e: {} for e in self.ENG}
        self.nres = 0
        self.dma_tokens = {}

    def res(self, name=None, track=True):
        self.nres += 1
        return Res("%s%d" % (name or "r", self.nres), track)

    def _need(self, eng, tok):
        if tok is None:
            return
        key, val = tok
        if self.waited[eng].get(key, 0) >= val:
            return
        self.waited[eng][key] = val
        sem = self.sems[key]
        self.prog[eng].append(lambda e, sem=sem, val=val: e.wait_ge(sem, val))

    def _deps(self, eng, reads, writes, skip_same=True):
        need = {}

        def add(t):
            if t is not None and need.get(t[0], 0) < t[1]:
                need[t[0]] = t[1]
        for r in reads:
            add(r.w)
        for w in writes:
            add(w.w)
            for t in w.r:
                if skip_same and t[0] == eng:
                    continue
                add(t)
        for k, v in need.items():
            self._need(eng, (k, v))

    def _commit(self, tok, reads, writes):
        for r in reads:
            r.r.append(tok)
            if len(r.r) > 24:
                best = {}
                for k, v in r.r:
                    if best.get(k, 0) < v:
                        best[k] = v
                r.r = list(best.items())
        for w in writes:
            w.w = tok
            w.r = []

    def op(self, eng, fn, reads=(), writes=()):
        fn = _record(fn)
        self._deps(eng, reads, writes)
        self.cnt[eng] += 1
        tok = (eng, self.cnt[eng])
        sem = self.sems[eng]
        self.prog[eng].append(lambda e, fn=fn, sem=sem: fn(e).then_inc(sem, 1))
        self._commit(tok, reads, writes)
        return tok

    def opn(self, eng, fn, reads=(), writes=()):
        fn = _record(fn)
        self._deps(eng, reads, writes)
        self.prog[eng].append(lambda e, fn=fn: fn(e))

    def dma(self, eng, fn, sres, reads=(), writes=()):
        fn = _record(fn)
        reads = [r for r in reads if r.track]
        writes = [w for w in writes if w.track]
        self._deps(eng, reads, writes, skip_same=False)
        if sres.dsem is None:
            key = "d%d_%s" % (len(self.sems), sres.name)
            sres.dsem = key
            self.sems[key] = self.stack.enter_context(self.nc.semaphore(key))
        sres.dcnt += 16
        tok = (sres.dsem, sres.dcnt)
        sem = self.sems[sres.dsem]
        self.prog[eng].append(lambda e, fn=fn, sem=sem: fn(e).then_inc(sem, 16))
        self.dma_tokens[sres.dsem] = sres.dcnt
        self._commit(tok, reads, writes)
        return tok

    def barrier(self):
        toks = [(k, v) for k, v in self.dma_tokens.items()]
        toks += [(e, self.cnt[e]) for e in ("pe", "act", "dve", "pool") if self.cnt[e]]
        for eng in self.ENG:
            for t in toks:
                if t[0] == eng:
                    continue
                self._need(eng, t)

    def emit(self):
        self.barrier()
        nc = self.nc
        with nc.Block() as block:
            def run(name):
                def body(e):
                    for f in self.prog[name]:
                        f(e)
                return body
            block.tensor(run("pe"))
            block.scalar(run("act"))
            block.vector(run("dve"))
            block.gpsimd(run("pool"))
            block.sync(run("sp"))


class Deferred:
    def __init__(self):
        self.q = []

    def add(self, n, fn):
        self.q.append([n, fn])

    def tick(self):
        for it in self.q:
            it[0] -= 1
        ready = [it for it in self.q if it[0] <= 0]
        self.q = [it for it in self.q if it[0] > 0]
        for it in ready:
            it[1]()

    def flush(self):
        while self.q:
            m = min(it[0] for it in self.q)
            for it in self.q:
                it[0] -= m
            self.tick() if m == 0 else self.tick_zero()

    def tick_zero(self):
        ready = [it for it in self.q if it[0] <= 0]
        self.q = [it for it in self.q if it[0] > 0]
        for it in ready:
            it[1]()


class Ring:
    def __init__(self, S, alloc, name, shape, dt, n):
        self.t = [alloc("%s%d" % (name, i), shape, dt) for i in range(n)]
        self.r = [S.res(name) for i in range(n)]
        self.i = -1
        self.n = n

    def next(self):
        self.i = (self.i + 1) % self.n
        return self.t[self.i], self.r[self.i]


class _Stop(Exception):
    pass


def build_program(nqt=16, stop=None):
    global NQT, S_TOK, N_OWN, NKP

    def stage(k):
        if stop is not None and stop == k:
            raise _Stop()

    NQT = nqt
    S_TOK = 512 * NQT
    N_OWN = 256 * NQT
    NKP = 128 * max(4 * NQT + 1, 10)
    nc = bass.Bass("TRN2", target_bir_lowering=False)
    di = lambda n, s, dt=F32: nc.dram_tensor(n, list(s), dt, kind="ExternalInput").ap()
    do = lambda n, s, dt=F32: nc.dram_tensor(n, list(s), dt, kind="ExternalOutput").ap()
    ds = lambda n, s, dt=BF16: nc.dram_tensor(n, list(s), dt, kind="Internal").ap()

    x_own = di("x_own", [N_OWN, 1024]); x_oth = di("x_oth", [N_OWN, 1024]); x_aux = di("x_aux", [128, 1024])
    cs_own = di("cs_own", [N_OWN, 32]); cs_oth = di("cs_oth", [N_OWN, 32]); cs_aux = di("cs_aux", [128, 32])
    cache_k = di("cache_k", [1024, 512]); cache_v = di("cache_v", [1024, 512]); state_conv = di("state_conv", [2, 512])
    w_in = di("w_in", [1024, 4096]); w_out = di("w_out", [1024, 1024]); ln_g = di("ln_g", [1024])
    qk_g = di("qk_g", [2, 64]); lam_in = di("lam_in", [4, 64]); subln_g = di("subln_g", [128])
    conv_w = di("conv_w", [3, 512]); conv_b = di("conv_b", [512])
    ident_d = di("ident", [128, 128], BF16); role_bias_d = di("role_bias", [128, 1])

    y_own = do("y_own", [N_OWN, 1024]); k_own = do("k_own", [N_OWN, 512]); v_own = do("v_own", [N_OWN, 512])
    k_aux = do("k_aux", [32, 512]); v_aux = do("v_aux", [32, 512]); y_s = do("y_s", [16, 1024])
    conv_p = do("conv_p", [2, 512]); conv_s = do("conv_s", [2, 512])

    kT_scr = ds("kT_scr", [128, 4, NKP]); v_scr = ds("v_scr", [NKP, 512])
    kTs_scr = ds("kTs_scr", [128, 4, NKS]); vs_scr = ds("vs_scr", [NKS, 512])
    NQC = N_OWN + 16
    qT_scr = ds("qT_scr", [128, 4, NQC]); sga_scr = ds("sga_scr", [128, 4, NQC]); mixc_scr = ds("mixc_scr", [128, 4, NQC])

    with ExitStack() as top:
        S = Sched(nc, top)
        r_kTs = S.res("kTs", False); r_vs = S.res("vs", False)
        r_qscr = [S.res("qscr", False) for _ in range(NQT + 1)]
        r_gscr = [S.res("gscr", False) for _ in range(NQT + 1)]
        r_mscr = [S.res("mscr", False) for _ in range(NQT + 1)]
        r_kblk = [S.res("kblk", False) for _ in range(NQT + 1)]

        try:
          with ExitStack() as p1:
              sb = lambda n, s, dt: p1.enter_context(nc.sbuf_tensor("a_" + n, list(s), dt))
              psum = lambda n, s, dt: p1.enter_context(nc.psum_tensor("a_" + n, list(s), dt))
              ring = lambda n, s, dt, k: Ring(S, sb, n, s, dt, k)

              wbf = sb("wbf", [128, 8, 4096], BF16); r_wbfA = S.res("wbfA"); r_wbfB = S.res("wbfB")
              ident = sb("ident", [128, 128], BF16); r_ident = S.res("ident")
              gt = sb("gt", [128, 8], F32); r_gt = S.res("gt")
              gqk = sb("gqk", [128, 2, 8, 64], F32); r_gqk = S.res("gqk")
              cw = sb("cw", [128, 4, 3], F32); cb = sb("cb", [128, 4], F32); r_cw = S.res("cw")
              mhalf = sb("mhalf", [128, 8], F32); r_mhalf = S.res("mhalf")
              uprev = sb("uprev", [128, 4, 32], F32); r_uprev = S.res("uprev")
              stT = sb("stT", [128, 4, 2], F32); r_stT = S.res("stT")
              utail = sb("utail", [128, 4, 2], F32); r_utail = S.res("utail")
              utail_s = sb("utail_s", [128, 4, 2], F32); r_utail_s = S.res("utail_s")

              xt_ring = ring("xt", [128, 1024], F32, 4)
              xs_ring = ring("xs", [128, 1024], F32, 2)
              xstat = [sb("xstat%d" % i, [128, 8], F32) for i in range(3)]; r_xstat = [S.res("xstat") for i in range(3)]
              junk = sb("junk", [128, 1024], BF16); r_junk = S.res("junk")
              st_ring = ring("stat", [128, 8], F32, 12)
              xn_ring = ring("xn", [128, 1024], BF16, 3)
              xnT_ring = ring("xnT", [128, 8, 512], BF16, 2)
              cs_ring = ring("cs", [128, 32], F32, 6)
              sq_ring = ring("sq", [128, 512], F32, 3)
              tq_ring = ring("tq", [128, 512], F32, 6)
              rt_ring = ring("rt", [128, 4, 8, 8], F32, 4)
              tb_ring = ring("tb", [128, 512], BF16, 4)
              vf_ring = ring("vf", [128, 512], F32, 2)
              vb_ring = ring("vb", [128, 512], BF16, 3)
              kst_ring = ring("kst", [128, 4, 512], BF16, 2)
              qst_ring = ring("qst", [128, 4, 512], BF16, 2)
              csb_ring = ring("csb", [128, 512], F32, 2)
              uext_ring = ring("uext", [128, 2, 258], F32, 2)
              ysb_ring = ring("ysb", [128, 512], F32, 2)
              sgc_ring = ring("sgc", [128, 512], F32, 2)
              gst_ring = ring("gst", [128, 4, 512], BF16, 2)
              mst_ring = ring("mst", [128, 4, 512], BF16, 2)

              pT = psum("pT", [128, 8, 128], BF16); r_pT = S.res("pT")
              pTq = psum("pTq", [128, 8, 128], BF16); r_pTq = S.res("pTq")
              pA = Ring(S, psum, "pA", [128, 512], F32, 6)

              S.dma("sp", lambda e: e.dma_start(out=gt[:], in_=ln_g.rearrange("(kt p) -> p kt", p=128),
                                                allow_slow_non_contiguous=True), r_gt, writes=[r_gt])
              S.dma("sp", lambda e: e.dma_start(out=ident[:], in_=ident_d), r_ident, writes=[r_ident])
              S.op("pool", lambda e: e.memset(mhalf[:], -0.5), writes=[r_mhalf])

              def col(v1d):
                  return v1d.rearrange("(p o) -> p o", o=1)

              def late_consts():
                  for a in range(2):
                      S.dma("sp", lambda e, a=a: e.dma_start(out=gqk[:, a, :, :],
                                                             in_=qk_g[a, :].partition_broadcast(128).unsqueeze(1).to_broadcast([128, 8, 64])),
                            r_gqk, writes=[r_gqk] if a == 1 else [])
                  for c in range(4):
                      for j in range(3):
                          S.dma("sp", lambda e: e.dma_start(out=cw[:, c, j:j + 1], in_=col(conv_w[j, c * 128:(c + 1) * 128])), r_cw)
                      S.dma("sp", lambda e: e.dma_start(out=cb[:, c:c + 1], in_=col(conv_b[c * 128:(c + 1) * 128])), r_cw,
                            writes=[r_cw] if c == 3 else [])
                      for t in range(2):
                          S.dma("sp", lambda e: e.dma_start(out=stT[:, c, t:t + 1], in_=col(state_conv[t, c * 128:(c + 1) * 128])), r_stT,
                                writes=[r_stT] if (c == 3 and t == 1) else [])

              def weight_chunks(kt):
                  for hh in range(4):
                      wt, rw = xt_ring.next()
                      S.dma("sp", lambda e: e.dma_start(out=wt[:], in_=w_in[kt * 128:(kt + 1) * 128, hh * 1024:(hh + 1) * 1024]), rw, writes=[rw])
                      if hh < 2:
                          S.op("dve", lambda e: e.tensor_scalar(out=wbf[:, kt, hh * 1024:(hh + 1) * 1024], in0=wt[:], scalar1=gt[:, kt:kt + 1],
                                                                scalar2=None, op0=ALU.mult), reads=[rw, r_gt], writes=[r_wbfA])
                      else:
                          S.op("act", lambda e: e.activation(out=wbf[:, kt, hh * 1024:(hh + 1) * 1024], in_=wt[:], func=AF.Copy,
                                                             scale=gt[:, kt:kt + 1]), reads=[rw, r_gt], writes=[r_wbfB])

              stage(1)
              def rsqrt_small(src_ap, n, mean_div, reads, out_res_pair):
                  o, ro = out_res_pair
                  S.op("dve", lambda e: e.tensor_scalar(out=o[:, 0:n], in0=src_ap, scalar1=1.0 / mean_div, scalar2=EPS,
                                                        op0=ALU.mult, op1=ALU.add), reads=reads, writes=[ro])
                  S.op("pool", lambda e: e.tensor_tensor(out=o[:, 0:n], in0=o[:, 0:n], in1=mhalf[:, 0:n], op=ALU.pow),
                       reads=[ro, r_mhalf], writes=[ro])
                  return o, ro

              def xload(src_rows):
                  xt, rx = xt_ring.next()
                  S.dma("sp", lambda e: e.dma_start(out=xt[:], in_=src_rows), rx, writes=[rx])
                  return xt, rx

              def xprep(src_rows, xnT, r_xnT, col0, loaded=None, rstd=None):
                  xt, rx = loaded if loaded is not None else xload(src_rows)
                  if rstd is None:
                      stt, rs_ = st_ring.next()
                      S.op("act", lambda e: e.activation(out=junk[:], in_=xt[:], func=AF.Square, accum_out=stt[:, 7:8]),
                           reads=[rx], writes=[r_junk, rs_])
                      S.op("dve", lambda e: e.tensor_scalar(out=stt[:, 0:1], in0=stt[:, 7:8], scalar1=1.0 / 1024, scalar2=EPS,
                                                            op0=ALU.mult, op1=ALU.add), reads=[rs_], writes=[rs_])
                      S.op("pool", lambda e: e.tensor_tensor(out=stt[:, 0:1], in0=stt[:, 0:1], in1=mhalf[:, 0:1], op=ALU.pow),
                           reads=[rs_, r_mhalf], writes=[rs_])
                      rstd = (stt[:, 0:1], rs_)
                  rstd_ap, rs_ = rstd
                  xn, rxn = xn_ring.next()
                  S.op("act", lambda e: e.activation(out=xn[:], in_=xt[:], func=AF.Copy, scale=rstd_ap),
                       reads=[rx, rs_], writes=[rxn])
                  for kt in range(8):
                      f = lambda e, kt=kt: e.transpose(out=pT[:, kt, :], in_=xn[:, kt * 128:(kt + 1) * 128], identity=ident[:])
                      if kt < 7:
                          S.opn("pe", f, reads=[rxn, r_ident], writes=[r_pT])
                      else:
                          S.op("pe", f, reads=[rxn, r_ident], writes=[r_pT])
                  S.op("dve", lambda e: e.tensor_copy(out=xnT[:, :, col0:col0 + 128], in_=pT[:]), reads=[r_pT], writes=[r_xnT])

              def proj_tok(xnT, r_xnT, col0, wcol0, bank=None):
                  pp, rp = pA.next()
                  for kt in range(8):
                      f = lambda e, kt=kt: e.matmul(pp[:], lhsT=xnT[:, kt, col0:col0 + 128], rhs=wbf[:, kt, wcol0:wcol0 + 512],
                                                    start=(kt == 0), stop=(kt == 7))
                      if kt < 7:
                          S.opn("pe", f, reads=[r_xnT, r_wbfA, r_wbfB], writes=[rp])
                      else:
                          S.op("pe", f, reads=[r_xnT, r_wbfA, r_wbfB], writes=[rp])
                  return pp, rp

              def qk_norm_rope(pp, rp, which, cs, rcs):
                  sq, rsq = sq_ring.next()
                  S.op("act", lambda e: e.activation(out=sq[:], in_=pp[:], func=AF.Square), reads=[rp], writes=[rsq])
                  stt, rs_ = st_ring.next()
                  S.op("dve", lambda e: e.tensor_reduce(out=stt[:, 0:8], in_=sq[:].rearrange("p (a b) -> p a b", b=64),
                                                        axis=AX.X, op=ALU.add), reads=[rsq], writes=[rs_])
                  S.op("dve", lambda e: e.tensor_scalar(out=stt[:, 0:8], in0=stt[:, 0:8], scalar1=1.0 / 64, scalar2=EPS,
                                                        op0=ALU.mult, op1=ALU.add), reads=[rs_], writes=[rs_])
                  S.op("pool", lambda e: e.tensor_tensor(out=stt[:, 0:8], in0=stt[:, 0:8], in1=mhalf[:, 0:8], op=ALU.pow),
                       reads=[rs_, r_mhalf], writes=[rs_])
                  t, rt_ = tq_ring.next()
                  t3 = t[:].rearrange("p (a b) -> p a b", b=64)
                  S.op("dve", lambda e: e.tensor_tensor(out=t3, in0=pp[:].rearrange("p (a b) -> p a b", b=64),
                                                        in1=stt[:, 0:8].unsqueeze(2).to_broadcast([128, 8, 64]), op=ALU.mult),
                       reads=[rp, rs_], writes=[rt_])
                  S.op("dve", lambda e: e.tensor_tensor(out=t3, in0=t3, in1=gqk[:, which, :, :], op=ALU.mult),
                       reads=[rt_, r_gqk], writes=[rt_])
                  tmp, rtm = rt_ring.next()
                  tmp2 = tmp[:].rearrange("p a b c -> p (a b c)").rearrange("p (a g d) -> p a g d", a=2, g=8)
                  ccb = cs[:, 0:16].unsqueeze(1).to_broadcast([128, 8, 16])
                  nsb = cs[:, 16:24].unsqueeze(1).to_broadcast([128, 8, 8])
                  psb = cs[:, 24:32].unsqueeze(1).to_broadcast([128, 8, 8])
                  x1 = t3[:, :, 0:8]; x2 = t3[:, :, 8:16]
                  S.op("pool", lambda e: e.tensor_tensor(out=tmp2[:, 0, :, :], in0=t3[:, :, 0:16], in1=ccb, op=ALU.mult), reads=[rt_, rcs], writes=[rtm])
                  S.op("pool", lambda e: e.tensor_tensor(out=tmp2[:, 1, :, 0:8], in0=x2, in1=nsb, op=ALU.mult), reads=[rt_, rcs], writes=[rtm])
                  S.op("pool", lambda e: e.tensor_tensor(out=tmp2[:, 1, :, 8:16], in0=x1, in1=psb, op=ALU.mult), reads=[rt_, rcs], writes=[rtm])
                  S.op("pool", lambda e: e.tensor_tensor(out=t3[:, :, 0:16], in0=tmp2[:, 0, :, :], in1=tmp2[:, 1, :, :], op=ALU.add), reads=[rtm], writes=[rt_])
                  return t, rt_

              def to_featmajor(t, rt_, half, dst, rdst, col0, ncols=128):
                  tb, rtb = tb_ring.next()
                  S.op("act", lambda e: e.activation(out=tb[:], in_=t[:], func=AF.Copy), reads=[rt_], writes=[rtb])
                  rq = r_pTq
                  for h in range(4):
                      f = lambda e, h=h: e.transpose(out=pTq[:, half * 4 + h, :], in_=tb[:, h * 128:(h + 1) * 128], identity=ident[:])
                      if h < 3:
                          S.opn("pe", f, reads=[rtb, r_ident], writes=[rq])
                      else:
                          S.op("pe", f, reads=[rtb, r_ident], writes=[rq])
                  S.op("dve", lambda e: e.tensor_copy(out=dst[:, :, col0:col0 + ncols], in_=pTq[:, half * 4:half * 4 + 4, 0:ncols]),
                       reads=[rq], writes=[rdst])

              def proj_feat(xnT, r_xnT, n, wcol0):
                  pf, rpf = pA.next()
                  for kt in range(8):
                      f = lambda e, kt=kt: e.matmul(pf[:, 0:n], lhsT=wbf[:, kt, wcol0:wcol0 + 128], rhs=xnT[:, kt, 0:n],
                                                    start=(kt == 0), stop=(kt == 7))
                      if kt < 7:
                          S.opn("pe", f, reads=[r_xnT, r_wbfA, r_wbfB], writes=[rpf])
                      else:
                          S.op("pe", f, reads=[r_xnT, r_wbfA, r_wbfB], writes=[rpf])
                  return pf, rpf

              def conv_chain(c, ue, rue, nseg, L, pB, rpB, sg, rsg, out_ap, rout):
                  ys, rys = ysb_ring.next()
                  y3 = ys[:, 0:nseg * L].rearrange("p (s l) -> p s l", l=L)
                  S.op("dve", lambda e: e.tensor_scalar(out=y3, in0=ue[:, 0:nseg, 0:L], scalar1=cw[:, c, 0:1], scalar2=cb[:, c:c + 1],
                                                        op0=ALU.mult, op1=ALU.add), reads=[rue, r_cw], writes=[rys])
                  S.op("dve", lambda e: e.scalar_tensor_tensor(out=y3, in0=ue[:, 0:nseg, 1:L + 1], scalar=cw[:, c, 1:2], in1=y3,
                                                               op0=ALU.mult, op1=ALU.add), reads=[rue, r_cw, rys], writes=[rys])
                  S.op("dve", lambda e: e.scalar_tensor_tensor(out=y3, in0=ue[:, 0:nseg, 2:L + 2], scalar=cw[:, c, 2:3], in1=y3,
                                                               op0=ALU.mult, op1=ALU.add), reads=[rue, r_cw, rys], writes=[rys])
                  S.op("dve", lambda e: e.tensor_tensor(out=y3, in0=pB, in1=y3, op=ALU.mult), reads=[rpB, rys], writes=[rys])
                  S.op("pool", lambda e: e.tensor_tensor(out=out_ap, in0=y3, in1=sg, op=ALU.mult), reads=[rys, rsg], writes=[rout])

              cst = {}

              def cache_tile(i):
                  ck, rck = tq_ring.next()
                  S.dma("sp", lambda e: e.dma_start(out=ck[:], in_=cache_k[i * 128:(i + 1) * 128, :]), rck, writes=[rck])
                  if i % 4 == 0:
                      cst["kst"] = kst_ring.next()
                  kst_c, rkst_c = cst["kst"]
                  to_featmajor(ck, rck, i % 2, kst_c, rkst_c, (i % 4) * 128)
                  if i % 4 == 3:
                      p0 = 128 + (i - 3) * 128
                      S.dma("sp", lambda e: e.dma_start(out=kTs_scr[:, :, p0:p0 + 512], in_=kst_c[:]), rkst_c, reads=[rkst_c], writes=[r_kTs])
                  cv, rcv = vf_ring.next()
                  S.dma("sp", lambda e: e.dma_start(out=cv[:], in_=cache_v[i * 128:(i + 1) * 128, :]), rcv, writes=[rcv])
                  vb, rvb = vb_ring.next()
                  S.op("dve", lambda e: e.tensor_copy(out=vb[:], in_=cv[:]), reads=[rcv], writes=[rvb])
                  S.dma("sp", lambda e: e.dma_start(out=vs_scr[128 + i * 128:128 + (i + 1) * 128, :], in_=vb[:]), rvb, reads=[rvb], writes=[r_vs])

              for kt in range(8):
                  weight_chunks(kt)
                  if kt == 0:
                      late_consts()
                  cache_tile(kt)

              xnTa, r_xnTa = xnT_ring.next()
              xprep(x_aux, xnTa, r_xnTa, 0)
              csa, rcsa = cs_ring.next()
              S.dma("sp", lambda e: e.dma_start(out=csa[:], in_=cs_aux), rcsa, writes=[rcsa])
              kst_a, rkst_a = kst_ring.next()
              qst_a, rqst_a = qst_ring.next()
              pq, rpq = proj_tok(xnTa, r_xnTa, 0, 0, 0)
              pk, rpk = proj_tok(xnTa, r_xnTa, 0, 512, 1)
              pv, rpv = proj_tok(xnTa, r_xnTa, 0, 1024, 2)
              tqa, rtqa = qk_norm_rope(pq, rpq, 0, csa, rcsa)
              tka, rtka = qk_norm_rope(pk, rpk, 1, csa, rcsa)
              S.dma("sp", lambda e: e.dma_start(out=k_aux, in_=tka[0:32, :]), rtka, reads=[rtka])
              vfa, rvfa = vf_ring.next()
              S.op("act", lambda e: e.activation(out=vfa[:], in_=pv[:], func=AF.Copy), reads=[rpv], writes=[rvfa])
              S.dma("sp", lambda e: e.dma_start(out=v_aux, in_=vfa[0:32, :]), rvfa, reads=[rvfa])
              vba, rvba = vb_ring.next()
              S.op("dve", lambda e: e.tensor_copy(out=vba[:], in_=vfa[:]), reads=[rvfa], writes=[rvba])
              S.dma("sp", lambda e: e.dma_start(out=v_scr[0:16, :], in_=vba[0:16, :]), rvba, reads=[rvba], writes=[r_kblk[NQT]])
              S.dma("sp", lambda e: e.dma_start(out=vs_scr[0:16, :], in_=vba[0:16, :]), rvba, reads=[rvba], writes=[r_vs])
              S.dma("sp", lambda e: e.dma_start(out=vs_scr[1152:1168, :], in_=vba[16:32, :]), rvba, reads=[rvba], writes=[r_vs])
              to_featmajor(tka, rtka, 0, kst_a, rkst_a, 0)
              to_featmajor(tqa, rtqa, 1, qst_a, rqst_a, 0)
              S.dma("sp", lambda e: e.dma_start(out=kT_scr[:, :, 0:16], in_=kst_a[:, :, 0:16]), rkst_a, reads=[rkst_a], writes=[r_kblk[NQT]])
              S.dma("sp", lambda e: e.dma_start(out=kTs_scr[:, :, 0:16], in_=kst_a[:, :, 0:16]), rkst_a, reads=[rkst_a], writes=[r_kTs])
              S.dma("sp", lambda e: e.dma_start(out=kTs_scr[:, :, 1152:1168], in_=kst_a[:, :, 16:32]), rkst_a, reads=[rkst_a], writes=[r_kTs])
              S.dma("sp", lambda e: e.dma_start(out=qT_scr[:, :, N_OWN:N_OWN + 16], in_=qst_a[:, :, 16:32]), rqst_a, reads=[rqst_a], writes=[r_qscr[NQT]])
              stage(2)
              gsa, rgsa = gst_ring.next()
              for h in range(4):
                  pf, rpf = proj_feat(xnTa, r_xnTa, 128, 1536 + h * 128)
                  S.op("act", lambda e, pf=pf, h=h: e.activation(out=gsa[:, h, 0:128], in_=pf[:, 0:128], func=AF.Silu), reads=[rpf], writes=[rgsa])
              S.dma("sp", lambda e: e.dma_start(out=sga_scr[:, :, N_OWN:N_OWN + 16], in_=gsa[:, :, 16:32]), rgsa, reads=[rgsa], writes=[r_gscr[NQT]])
              msa, rmsa = mst_ring.next()
              for c in range(4):
                  pf, rpf = proj_feat(xnTa, r_xnTa, 128, 2560 + c * 128)
                  csb, rcsb = csb_ring.next()
                  S.op("act", lambda e, pf=pf, csb=csb: e.activation(out=csb[:, 0:128], in_=pf[:, 0:128], func=AF.Copy), reads=[rpf], writes=[rcsb])
                  pf, rpf = proj_feat(xnTa, r_xnTa, 128, 3072 + c * 128)
                  ue, rue = uext_ring.next()
                  ua = sq_ring.next()
                  S.op("dve", lambda e, pf=pf, csb=csb, ua=ua: e.tensor_tensor(out=ua[0][:, 0:128], in0=pf[:, 0:128], in1=csb[:, 0:128], op=ALU.mult),
                       reads=[rpf, rcsb], writes=[ua[1]])
                  S.op("pool", lambda e, ua=ua, c=c: e.tensor_copy(out=uprev[:, c, :], in_=ua[0][:, 32:64]), reads=[ua[1]], writes=[r_uprev])
                  S.op("pool", lambda e, ua=ua, ue=ue: e.tensor_copy(out=ue[:, 0, 2:18], in_=ua[0][:, 16:32]), reads=[ua[1]], writes=[rue])
                  S.op("pool", lambda e, ue=ue, c=c: e.tensor_copy(out=ue[:, 0, 0:2], in_=stT[:, c, :]), reads=[r_stT], writes=[rue])
                  S.op("pool", lambda e, ue=ue, c=c: e.tensor_copy(out=utail_s[:, c, :], in_=ue[:, 0, 16:18]), reads=[rue], writes=[r_utail_s])
                  pfB, rpfB = proj_feat(xnTa, r_xnTa, 128, 2048 + c * 128)
                  pfG, rpfG = proj_feat(xnTa, r_xnTa, 128, 3584 + c * 128)
                  sg, rsg = sgc_ring.next()
                  S.op("act", lambda e, pfG=pfG, sg=sg: e.activation(out=sg[:, 0:16], in_=pfG[:, 16:32], func=AF.Silu), reads=[rpfG], writes=[rsg])
                  conv_chain(c, ue, rue, 1, 16, pfB[:, 16:32].unsqueeze(1), rpfB, sg[:, 0:16].unsqueeze(1), rsg,
                             msa[:, c, 0:16].unsqueeze(1), rmsa)
              S.dma("sp", lambda e: e.dma_start(out=mixc_scr[:, :, N_OWN:N_OWN + 16], in_=msa[:, :, 0:16]), rmsa, reads=[rmsa], writes=[r_mscr[NQT]])
              for c in range(4):
                  for t in range(2):
                      S.dma("sp", lambda e: e.dma_start(out=col(conv_s[t, c * 128:(c + 1) * 128]), in_=utail_s[:, c, t:t + 1]),
                            r_utail_s, reads=[r_utail_s])

              stage(3)
              stage(4)
              tiles = [(J, own, t) for J in range(NQT // 2) for own in (True, False) for t in range(4)]
              T = [dict() for _ in tiles]
              sup = {}

              def st_xload(n):
                  J, own, t = tiles[n]
                  row0 = J * 512 + t * 128
                  xsrc = x_own if own else x_oth
                  cssrc = cs_own if own else cs_oth
                  T[n]["xl"] = xload(xsrc[row0:row0 + 128, :])
                  cs, rcs = cs_ring.next()
                  S.dma("sp", lambda e: e.dma_start(out=cs[:], in_=cssrc[row0:row0 + 128, :]), rcs, writes=[rcs])
                  T[n].update(cs=cs, rcs=rcs, row0=row0)

              supers = [(J, own) for J in range(NQT // 2) for own in (True, False)]

              stats_pending = {}

              def st_stats_load(si, t):
                  J, own = supers[si]
                  row0 = J * 512 + t * 128
                  xsrc = x_own if own else x_oth
                  xs, rxs = xs_ring.next()
                  S.dma("act", lambda e: e.dma_start(out=xs[:], in_=xsrc[row0:row0 + 128, :]), rxs, writes=[rxs])
                  stats_pending[(si, t)] = (xs, rxs)

              def st_stats_sq(si, t):
                  xs, rxs = stats_pending.pop((si, t))
                  st_ = xstat[si % 3]
                  S.op("act", lambda e: e.activation(out=junk[:], in_=xs[:], func=AF.Square, accum_out=st_[:, 4 + t:5 + t]),
                       reads=[rxs], writes=[r_junk, r_xstat[si % 3]])

              def st_stats(si, t):
                  st_stats_load(si, t)
                  st_stats_sq(si, t)

              def st_stats_finish(si):
                  st_ = xstat[si % 3]
                  rr = r_xstat[si % 3]
                  S.op("dve", lambda e: e.tensor_scalar(out=st_[:, 0:4], in0=st_[:, 4:8], scalar1=1.0 / 1024, scalar2=EPS,
                                                        op0=ALU.mult, op1=ALU.add), reads=[rr], writes=[rr])
                  S.op("pool", lambda e: e.tensor_tensor(out=st_[:, 0:4], in0=st_[:, 0:4], in1=mhalf[:, 0:4], op=ALU.pow),
                       reads=[rr, r_mhalf], writes=[rr])

              def st_xprep(n):
                  J, own, t = tiles[n]
                  if t == 0:
                      sup[(J, own)] = dict(xnT=xnT_ring.next())
                  xnT, r_xnT = sup[(J, own)]["xnT"]
                  si = supers.index((J, own))
                  xprep(None, xnT, r_xnT, t * 128, loaded=T[n]["xl"], rstd=(xstat[si % 3][:, t:t + 1], r_xstat[si % 3]))

              def st_proj(n):
                  J, own, t = tiles[n]
                  xnT, r_xnT = sup[(J, own)]["xnT"]
                  T[n]["pk"] = proj_tok(xnT, r_xnT, t * 128, 512)
                  T[n]["pv"] = proj_tok(xnT, r_xnT, t * 128, 1024)
                  if own:
                      T[n]["pq"] = proj_tok(xnT, r_xnT, t * 128, 0)

              def st_chain(n):
                  J, own, t = tiles[n]
                  d = T[n]
                  cs, rcs, row0 = d["cs"], d["rcs"], d["row0"]
                  pk, rpk = d["pk"]
                  pv, rpv = d["pv"]
                  tk, rtk = qk_norm_rope(pk, rpk, 1, cs, rcs)
                  if own:
                      S.dma("sp", lambda e: e.dma_start(out=k_own[row0:row0 + 128, :], in_=tk[:]), rtk, reads=[rtk])
                  vf, rvf = vf_ring.next()
                  S.op("act", lambda e: e.activation(out=vf[:], in_=pv[:], func=AF.Copy), reads=[rpv], writes=[rvf])
                  if own:
                      S.dma("sp", lambda e: e.dma_start(out=v_own[row0:row0 + 128, :], in_=vf[:]), rvf, reads=[rvf])
                  vb, rvb = vb_ring.next()
                  S.op("dve", lambda e: e.tensor_copy(out=vb[:], in_=vf[:]), reads=[rvf], writes=[rvb])
                  j = 2 * J + t // 2
                  pos = 128 + 512 * j + (0 if own else 256) + (t % 2) * 128
                  S.dma("sp", lambda e: e.dma_start(out=v_scr[pos:pos + 128, :], in_=vb[:]), rvb, reads=[rvb], writes=[r_kblk[j]])
                  d.update(tk=tk, rtk=rtk)
                  if own:
                      pq, rpq = d["pq"]
                      tq, rtq = qk_norm_rope(pq, rpq, 0, cs, rcs)
                      d.update(tq=tq, rtq=rtq)

              def st_featT(n):
                  J, own, t = tiles[n]
                  d = T[n]
                  su = sup[(J, own)]
                  if t == 0:
                      su["kst"] = kst_ring.next()
                      if own:
                          su["qst"] = qst_ring.next()
                  kst, rkst = su["kst"]
                  to_featmajor(d["tk"], d["rtk"], 0, kst, rkst, t * 128)
                  if own:
                      qst, rqst = su["qst"]
                      to_featmajor(d["tq"], d["rtq"], 1, qst, rqst, t * 128)
                  if t == 3:
                      for s_ in range(2):
                          j = 2 * J + s_
                          pos = 128 + 512 * j + (0 if own else 256)
                          S.dma("sp", lambda e: e.dma_start(out=kT_scr[:, :, pos:pos + 256], in_=kst[:, :, s_ * 256:(s_ + 1) * 256]),
                                rkst, reads=[rkst], writes=[r_kblk[j]])
                      if own:
                          S.dma("sp", lambda e: e.dma_start(out=qT_scr[:, :, J * 512:(J + 1) * 512], in_=qst[:]), rqst,
                                reads=[rqst], writes=[r_qscr[2 * J], r_qscr[2 * J + 1]])
                  T[n].clear()

              def st_fm(J):
                  xnT, r_xnT = sup[(J, True)]["xnT"]
                  gs, rgs = gst_ring.next()
                  for h in range(4):
                      pf, rpf = proj_feat(xnT, r_xnT, 512, 1536 + h * 128)
                      S.op("act", lambda e: e.activation(out=gs[:, h, :], in_=pf[:], func=AF.Silu), reads=[rpf], writes=[rgs])
                  S.dma("sp", lambda e: e.dma_start(out=sga_scr[:, :, J * 512:(J + 1) * 512], in_=gs[:]), rgs,
                        reads=[rgs], writes=[r_gscr[2 * J], r_gscr[2 * J + 1]])
                  ms, rms = mst_ring.next()
                  for c in range(4):
                      pfC, rpfC = proj_feat(xnT, r_xnT, 512, 2560 + c * 128)
                      csb, rcsb = csb_ring.next()
                      S.op("act", lambda e: e.activation(out=csb[:], in_=pfC[:], func=AF.Copy), reads=[rpfC], writes=[rcsb])
                      pfH, rpfH = proj_feat(xnT, r_xnT, 512, 3072 + c * 128)
                      ue, rue = uext_ring.next()
                      S.op("dve", lambda e: e.tensor_tensor(
                          out=ue[:, :, 2:258], in0=pfH[:].rearrange("p (s l) -> p s l", l=256),
                          in1=csb[:].rearrange("p (s l) -> p s l", l=256), op=ALU.mult), reads=[rpfH, rcsb], writes=[rue])
                      for s_ in range(2):
                          j = 2 * J + s_
                          S.op("pool", lambda e: e.tensor_copy(out=ue[:, s_, 0:2], in_=uprev[:, c, 2 * j:2 * j + 2]),
                               reads=[r_uprev], writes=[rue])
                      if J == NQT // 2 - 1:
                          S.op("pool", lambda e: e.tensor_copy(out=utail[:, c, :], in_=ue[:, 1, 256:258]), reads=[rue], writes=[r_utail])
                      pfB, rpfB = proj_feat(xnT, r_xnT, 512, 2048 + c * 128)
                      pfG, rpfG = proj_feat(xnT, r_xnT, 512, 3584 + c * 128)
                      sg, rsg = sgc_ring.next()
                      S.op("act", lambda e: e.activation(out=sg[:], in_=pfG[:], func=AF.Silu), reads=[rpfG], writes=[rsg])
                      conv_chain(c, ue, rue, 2, 256, pfB[:].rearrange("p (s l) -> p s l", l=256), rpfB,
                                 sg[:].rearrange("p (s l) -> p s l", l=256), rsg,
                                 ms[:, c, :].rearrange("p (s l) -> p s l", l=256), rms)
                  S.dma("sp", lambda e: e.dma_start(out=mixc_scr[:, :, J * 512:(J + 1) * 512], in_=ms[:]), rms,
                        reads=[rms], writes=[r_mscr[2 * J], r_mscr[2 * J + 1]])

              NT = len(tiles)
              for si0 in range(min(2, len(supers))):
                  for t_ in range(4):
                      st_stats(si0, t_)
                  st_stats_finish(si0)
              st_xload(0)
              st_xload(1)
              st_xload(2)
              st_xprep(0)
              st_xprep(1)
              for n in range(NT):
                  J_, own_, t_ = tiles[n]
                  si_ = supers.index((J_, own_))
                  do_stats = si_ + 2 < len(supers) and t_ < 2
                  if do_stats:
                      st_stats_load(si_ + 2, 2 * t_)
                      st_stats_load(si_ + 2, 2 * t_ + 1)
                  if n + 3 < NT:
                      st_xload(n + 3)
                  if n + 2 < NT:
                      st_xprep(n + 2)
                  st_proj(n)
                  if n >= 2:
                      st_featT(n - 2)
                  st_chain(n)
                  if do_stats:
                      st_stats_sq(si_ + 2, 2 * t_)
                      st_stats_sq(si_ + 2, 2 * t_ + 1)
                      if t_ == 1:
                          st_stats_finish(si_ + 2)
                  J, own, t = tiles[n]
                  if own and t == 3:
                      st_fm(J)
              st_featT(NT - 2)
              st_featT(NT - 1)
              for c in range(4):
                  for t in range(2):
                      S.dma("sp", lambda e: e.dma_start(out=col(conv_p[t, c * 128:(c + 1) * 128]), in_=utail[:, c, t:t + 1]),
                            r_utail, reads=[r_utail])
              S.barrier()

          stage(5)
          with ExitStack() as p2:
              sb = lambda n, s, dt: p2.enter_context(nc.sbuf_tensor("b_" + n, list(s), dt))
              psum = lambda n, s, dt: p2.enter_context(nc.psum_tensor("b_" + n, list(s), dt))
              ring = lambda n, s, dt, k: Ring(S, sb, n, s, dt, k)

              KT = sb("KT", [128, 4, NKP], BF16)
              Vs = sb("Vs", [128, NKP // 128, 512], BF16)
              r_kt = [S.res("ktK") for _ in range(NKP // 128)]
              r_ktv = [S.res("ktV") for _ in range(NKP // 128)]
              wobf = sb("wobf", [128, 8, 1024], BF16); r_wobf = S.res("wobf")
              ones = sb("ones", [128, 128], BF16); r_ones = S.res("ones")
              rbias = sb("rbias", [128, 1], F32); r_rbias = S.res("rbias")
              lamt = sb("lamt", [128, 4, 64], F32); r_lamt = S.res("lamt")
              lamp = sb("lamp", [128, 2, 64], F32); r_lamp = S.res("lamp")
              lams = sb("lams", [128, 4], F32); r_lams = S.res("lams")
              gsub = sb("gsub", [128, 1], F32); r_gsub = S.res("gsub")
              wo_ring = ring("wost", [128, 256], F32, 2)
              qt_ring = ring("QT", [128, 4, 2, 256], BF16, 2)
              ones16 = sb("ones16", [128, 128], BF16); r_ones16 = S.res("ones16")
              sga_ring = ring("SGA", [128, 4, 256], BF16, 2)
              mix_ring = ring("MIX", [128, 8, 256], BF16, 2)
              e_ring = ring("E", [128, 512], BF16, 5)
              rsb_ring = ring("Rsb", [128, 512], F32, 2)
              tsb_ring = ring("Tsb", [128, 512], F32, 1)
              osb_ring = ring("Osb", [128, 256], F32, 3)
              o2_ring = ring("O2", [128, 256], F32, 1)
              sqb_ring = ring("sqb", [128, 256], BF16, 3)
              vv_ring = ring("vv", [128, 256], F32, 3)
              xres_ring = ring("xres", [128, 512], F32, 2)
              yo_ring = ring("yo", [128, 512], F32, 2)
              acc_ring = ring("acc", [128, 512], F32, 2)
              hl_ring = ring("hl", [128, 2, 512], BF16, 2)

              pS = Ring(S, psum, "pS", [128, 512], F32, 3)
              pO = Ring(S, psum, "pO", [128, 512], F32, 2)
              pD = Ring(S, psum, "pD", [128, 512], F32, 2)
              pY = psum("pY", [128, 512], F32); r_pY = S.res("pY")
              pSS, r_pSS = pY, r_pY

              S.op("pool", lambda e: e.memset(ones[:], 1.0), writes=[r_ones])
              S.op("pool", lambda e: e.memset(ones16[:], 0.0), writes=[r_ones16])
              S.op("pool", lambda e: e.memset(ones16[0:16, :], 1.0), writes=[r_ones16])
              for _i in range(2):
                  S.op("pool", lambda e: e.memset(qt_ring.t[_i][:], 0.0), writes=[qt_ring.r[_i]])
              S.dma("sp", lambda e: e.dma_start(out=rbias[:], in_=role_bias_d), r_rbias, writes=[r_rbias])
              S.dma("sp", lambda e: e.dma_start(out=lamt[:].rearrange("p a b -> p (a b)"),
                                                in_=lam_in.rearrange("a b -> (a b)").partition_broadcast(128)), r_lamt, writes=[r_lamt])
              S.dma("sp", lambda e: e.dma_start(out=gsub[:], in_=subln_g.rearrange("(p o) -> p o", o=1)), r_gsub, writes=[r_gsub])
              S.op("dve", lambda e: e.tensor_scalar(out=gsub[:], in0=gsub[:], scalar1=1.0 - LI, scalar2=None, op0=ALU.mult),
                   reads=[r_gsub], writes=[r_gsub])
              S.op("dve", lambda e: e.tensor_tensor(out=lamp[:], in0=lamt[:, 0:4:2, :], in1=lamt[:, 1:4:2, :], op=ALU.mult),
                   reads=[r_lamt], writes=[r_lamp])
              S.op("dve", lambda e: e.tensor_reduce(out=lams[:, 0:2], in_=lamp[:], axis=AX.X, op=ALU.add), reads=[r_lamp], writes=[r_lams])
              S.op("act", lambda e: e.activation(out=lams[:, 0:2], in_=lams[:, 0:2], func=AF.Exp), reads=[r_lams], writes=[r_lams])
              S.op("dve", lambda e: e.tensor_tensor(out=lams[:, 2:3], in0=lams[:, 1:2], in1=lams[:, 0:1], op=ALU.subtract),
                   reads=[r_lams], writes=[r_lams])
              S.op("dve", lambda e: e.tensor_scalar(out=lams[:, 3:4], in0=lams[:, 2:3], scalar1=-LI, scalar2=None, op0=ALU.add),
                   reads=[r_lams], writes=[r_lams])
              neglam = lams[:, 3:4]
              def wout_chunk(c, qq):
                  wt, rw = wo_ring.next()
                  S.dma("sp", lambda e: e.dma_start(out=wt[:], in_=w_out[c * 128:(c + 1) * 128, qq * 256:(qq + 1) * 256]), rw, writes=[rw])
                  S.op("dve", lambda e: e.tensor_copy(out=wobf[:, c, qq * 256:(qq + 1) * 256], in_=wt[:]), reads=[rw], writes=[r_wobf])

              stage(6)
              def load_keys(src_kT, src_v, rsrc, p0, npos, tiles):
                  if npos == 16:
                      t0 = tiles[0]
                      S.op("pool", lambda e: e.memset(KT[:, :, p0:p0 + 128], 0.0), writes=[r_kt[t0]])
                      S.op("pool", lambda e: e.memset(Vs[:, t0, :], 0.0), writes=[r_ktv[t0]])
                  S.dma("sp", lambda e: e.dma_start(out=KT[:, :, p0:p0 + npos], in_=src_kT[:, :, p0:p0 + npos]), r_kt[tiles[0]],
                        reads=rsrc, writes=[r_kt[t] for t in tiles])
                  if npos == 16:
                      S.dma("sp", lambda e: e.dma_start(out=Vs[0:16, t0, :], in_=src_v[p0:p0 + 16, :]), r_ktv[t0], reads=rsrc, writes=[r_ktv[t0]])
                  else:
                      nt = len(tiles)
                      S.dma("sp", lambda e: e.dma_start(out=Vs[:, tiles[0]:tiles[0] + nt, :],
                                                        in_=src_v[p0:p0 + npos, :].rearrange("(t p) c -> p t c", p=128)),
                            r_ktv[tiles[0]], reads=rsrc, writes=[r_ktv[t] for t in tiles])

              def load_qtile(j, nq, xsrc_rows):
                  c0 = j * 256
                  qt, rqt = qt_ring.next()
                  S.dma("sp", lambda e: e.dma_start(out=qt[0:64, :, 0, 0:nq], in_=qT_scr[0:64, :, c0:c0 + nq]), rqt, reads=[r_qscr[j]])
                  S.dma("sp", lambda e: e.dma_start(out=qt[64:128, :, 1, 0:nq], in_=qT_scr[64:128, :, c0:c0 + nq]), rqt, reads=[r_qscr[j]], writes=[rqt])
                  sg, rsg = sga_ring.next()
                  S.dma("sp", lambda e: e.dma_start(out=sg[:, :, 0:nq], in_=sga_scr[:, :, c0:c0 + nq]), rsg, reads=[r_gscr[j]], writes=[rsg])
                  mx, rmx = mix_ring.next()
                  if nq < 128:
                      S.op("pool", lambda e: e.memset(mx[:], 0.0), writes=[rmx])
                  S.dma("sp", lambda e: e.dma_start(out=mx[:, 4:8, 0:nq], in_=mixc_scr[:, :, c0:c0 + nq]), rmx, reads=[r_mscr[j]], writes=[rmx])
                  return (qt, rqt, sg, rsg, mx, rmx)

              defer = Deferred()

              def attention(nq, bufs, key_tiles, next_nkt=99):
                  qt, rqt, sg, rsg, mx, rmx = bufs
                  n2 = 2 * nq
                  for h in range(4):
                      O, rO = pO.next()
                      acc, racc = acc_ring.next()
                      nkt = len(key_tiles)
                      Sbufs = [None] * nkt

                      def qk(i):
                          kt, nk, mask = key_tiles[i]
                          kp = 128 * kt
                          Sb, rS = pS.next()
                          Sbufs[i] = (Sb, rS)
                          if mask == "diagB":
                              S.opn("pe", lambda e: e.matmul(Sb[:, 128:nq], lhsT=KT[:, h, kp:kp + 128], rhs=qt[:, h, 0, 128:nq],
                                                             start=True, stop=True), reads=[r_kt[kt], rqt], writes=[rS])
                              S.op("pe", lambda e: e.matmul(Sb[:, nq + 128:n2], lhsT=KT[:, h, kp:kp + 128], rhs=qt[:, h, 1, 128:nq],
                                                            start=True, stop=True), reads=[r_kt[kt], rqt], writes=[rS])
                              return
                          S.op("pe", lambda e: e.matmul(Sb[:, 0:n2].rearrange("p (c q) -> p c q", c=2), lhsT=KT[:, h, kp:kp + 128],
                                                        rhs=qt[:, h, :, 0:nq], start=True, stop=True), reads=[r_kt[kt], rqt], writes=[rS])

                      qk(0)
                      if nkt > 1:
                          qk(1)
                      D, rD = pD.next()
                      dst = {"started": False}
                      for i in range(nkt):
                          kt, nk, mask = key_tiles[i]
                          Sb, rS = Sbufs[i]
                          E, rE = e_ring.next()
                          E3 = E[:, 0:n2].rearrange("p (c q) -> p c q", c=2)
                          S3 = Sb[:, 0:n2].rearrange("p (c q) -> p c q", c=2)
                          O3 = O[:, 0:n2].rearrange("p (c q) -> p c q", c=2)
                          A3 = acc[:, 0:n2].rearrange("p (c q) -> p c q", c=2)
                          q0 = 128 if mask == "diagB" else 0
                          if mask == "diagB":
                              S.op("act", lambda e: e.activation(out=E3[:, :, q0:nq], in_=S3[:, :, q0:nq], func=AF.Exp, scale=0.125),
                                   reads=[rS], writes=[rE])
                          elif mask == "role":
                              S.op("act", lambda e: e.activation(out=E[:, 0:n2], in_=Sb[:, 0:n2], func=AF.Exp, scale=0.125,
                                                                 bias=rbias[:, :]), reads=[rS, r_rbias], writes=[rE])
                          else:
                              S.op("act", lambda e: e.activation(out=E[:, 0:n2], in_=Sb[:, 0:n2], func=AF.Exp, scale=0.125),
                                   reads=[rS], writes=[rE])
                          if mask == "diagA":
                              S.op("act", lambda e: e.activation(out=E3[64:128, :, 0:64], in_=S3[64:128, :, 0:64], func=AF.Copy, scale=0.0),
                                   reads=[rS], writes=[rE])
                          elif mask == "diagB":
                              S.op("act", lambda e: e.activation(out=E3[64:128, :, 128:192], in_=S3[64:128, :, 128:192], func=AF.Copy, scale=0.0),
                                   reads=[rS], writes=[rE])
                          if i + 2 < nkt:
                              qk(i + 2)
                          if mask == "diagB":
                              S.opn("pe", lambda e: e.matmul(O[:, 128:nq], lhsT=Vs[:, kt, h * 128:(h + 1) * 128], rhs=E[:, 128:nq],
                                                             start=False, stop=False), reads=[r_ktv[kt], rE], writes=[rO])
                              S.op("pe", lambda e: e.matmul(O[:, nq + 128:n2], lhsT=Vs[:, kt, h * 128:(h + 1) * 128], rhs=E[:, nq + 128:n2],
                                                            start=False, stop=False), reads=[r_ktv[kt], rE], writes=[rO])
                          else:
                              S.op("pe", lambda e: e.matmul(O[:, 0:n2], lhsT=Vs[:, kt, h * 128:(h + 1) * 128], rhs=E[:, 0:n2],
                                                            start=(i == 0), stop=(i == nkt - 1)), reads=[r_ktv[kt], rE], writes=[rO])
                          if i == 0:
                              S.op("dve", lambda e: e.memset(acc[:, 0:n2], 0.0), writes=[racc])
                          if i % 2 == 0 or nk != 128 or mask == "diagB":
                              S.op("dve", lambda e: e.tensor_tensor(out=A3[0:nk, :, q0:nq], in0=A3[0:nk, :, q0:nq], in1=E3[0:nk, :, q0:nq], op=ALU.add),
                                   reads=[rE, racc], writes=[racc])
                          else:
                              S.op("pe", lambda e: e.matmul(D[:, 0:n2], lhsT=ones[:, :], rhs=E[:, 0:n2], start=(not dst["started"]), stop=False),
                                   reads=[r_ones, rE], writes=[rD])
                              dst["started"] = True
                          defer.tick()
                      finish_head(h, nq, O, rO, acc, racc, sg, rsg, mx, rmx, D, rD, dst["started"], min(nkt, nkt if h < 3 else next_nkt))

              def finish_head(h, nq, O, rO, acc, racc, sg, rsg, mx, rmx, D, rD, d_started, nkt_head):
                  n2 = 2 * nq
                  st = {}

                  def f_a0():
                      hl, rhl = hl_ring.next()
                      S.op("dve", lambda e: e.tensor_copy(out=hl[:, 0, 0:n2], in_=acc[:, 0:n2]), reads=[racc], writes=[rhl])
                      S.op("dve", lambda e: e.tensor_tensor(out=hl[:, 1, 0:n2], in0=acc[:, 0:n2], in1=hl[:, 0, 0:n2], op=ALU.subtract),
                           reads=[racc, rhl], writes=[rhl])
                      st.update(hl=hl, rhl=rhl)

                  def f_a1():
                      hl, rhl = st["hl"], st["rhl"]
                      S.opn("pe", lambda e: e.matmul(D[:, 0:n2], lhsT=ones[:, :], rhs=hl[:, 0, 0:n2], start=(not d_started), stop=False),
                            reads=[r_ones, rhl], writes=[rD])
                      S.op("pe", lambda e: e.matmul(D[:, 0:n2], lhsT=ones[:, :], rhs=hl[:, 1, 0:n2], start=False, stop=True),
                           reads=[r_ones, rhl], writes=[rD])

                  def f_a2():
                      Rs, rRs = rsb_ring.next()
                      S.op("act", lambda e: e.activation(out=Rs[:, 0:n2], in_=D[:, 0:n2], func=AF.Ln), reads=[rD], writes=[rRs])
                      S.op("act", lambda e: e.activation(out=Rs[:, 0:n2], in_=Rs[:, 0:n2], func=AF.Exp, scale=-1.0), reads=[rRs], writes=[rRs])
                      st.update(Rs=Rs, rRs=rRs)

                  def f_a3():
                      Rs, rRs = st["Rs"], st["rRs"]
                      Ts, rTs = tsb_ring.next()
                      S.op("dve", lambda e: e.tensor_tensor(out=Ts[:, 0:n2], in0=O[:, 0:n2], in1=Rs[:, 0:n2], op=ALU.mult),
                           reads=[rO, rRs], writes=[rTs])
                      Os, rOs = osb_ring.next()
                      S.op("dve", lambda e: e.scalar_tensor_tensor(out=Os[:, 0:nq], in0=Ts[:, nq:n2], scalar=neglam, in1=Ts[:, 0:nq],
                                                                   op0=ALU.mult, op1=ALU.add), reads=[rTs, r_lams], writes=[rOs])
                      st.update(Os=Os, rOs=rOs)

                  def f_a4():
                      Os, rOs = st["Os"], st["rOs"]
                      sqb, rsqb = sqb_ring.next()
                      S.op("act", lambda e: e.activation(out=sqb[:, 0:nq], in_=Os[:, 0:nq], func=AF.Square), reads=[rOs], writes=[rsqb])
                      st.update(sqb=sqb, rsqb=rsqb)

                  def f_b1():
                      sqb, rsqb = st["sqb"], st["rsqb"]
                      S.op("pe", lambda e: e.matmul(pSS[:, 0:nq], lhsT=ones[:], rhs=sqb[:, 0:nq], start=True, stop=True),
                           reads=[r_ones, rsqb], writes=[r_pSS])

                  def f_b2():
                      vv, rvv = vv_ring.next()
                      S.op("dve", lambda e: e.tensor_scalar(out=vv[:, 0:nq], in0=pSS[:, 0:nq], scalar1=1.0 / 128, scalar2=EPS,
                                                            op0=ALU.mult, op1=ALU.add), reads=[r_pSS], writes=[rvv])
                      st.update(vv=vv, rvv=rvv)

                  def f_b3():
                      vv, rvv = st["vv"], st["rvv"]
                      S.op("act", lambda e: e.activation(out=vv[:, 0:nq], in_=vv[:, 0:nq], func=AF.Ln), reads=[rvv], writes=[rvv])
                      S.op("act", lambda e: e.activation(out=vv[:, 0:nq], in_=vv[:, 0:nq], func=AF.Exp, scale=-0.5), reads=[rvv], writes=[rvv])

                  def f_c():
                      Os, rOs, vv, rvv = st["Os"], st["rOs"], st["vv"], st["rvv"]
                      O2, rO2 = o2_ring.next()
                      S.op("dve", lambda e: e.tensor_tensor(out=O2[:, 0:nq], in0=Os[:, 0:nq], in1=vv[:, 0:nq], op=ALU.mult),
                           reads=[rOs, rvv], writes=[rO2])
                      S.op("dve", lambda e: e.scalar_tensor_tensor(out=mx[:, h, 0:nq], in0=O2[:, 0:nq], scalar=gsub[:, 0:1], in1=sg[:, h, 0:nq],
                                                                   op0=ALU.mult, op1=ALU.mult), reads=[rO2, r_gsub, rsg], writes=[rmx])

                  def f_b12():
                      f_b1()
                      f_b2()

                  if nkt_head >= 9:
                      sched = ((1, f_a0), (3, f_a1), (5, f_a2), (7, f_a3), (9, f_a4), (11, f_b12), (14, f_b3), (16, f_c))
                  else:
                      sched = ((1, f_a0), (2, f_a1), (3, f_a2), (4, f_a3), (5, f_a4), (7, f_b12), (9, f_b3), (11, f_c))
                  for dly, fn in sched:
                      defer.add(dly, fn)

              def out_proj_q(nq, bufs, xsrc, ydst, th, ch):
                  qt, rqt, sg, rsg, mx, rmx = bufs
                  nt = min(128, nq)
                  mt = 128
                  xr, rxr = xres_ring.next()
                  S.dma("sp", lambda e: e.dma_start(out=xr[0:nt, :], in_=xsrc[th * 128:th * 128 + nt, ch * 512:(ch + 1) * 512]),
                        rxr, writes=[rxr])
                  yo, ryo = yo_ring.next()
                  for c in range(8):
                      f = lambda e, c=c: e.matmul(pY[0:mt, :], lhsT=mx[:, c, th * 128:th * 128 + mt],
                                                  rhs=wobf[:, c, ch * 512:(ch + 1) * 512], start=(c == 0), stop=(c == 7))
                      if c < 7:
                          S.opn("pe", f, reads=[rmx, r_wobf], writes=[r_pY])
                      else:
                          S.op("pe", f, reads=[rmx, r_wobf], writes=[r_pY])
                  S.op("dve", lambda e: e.tensor_tensor(out=yo[0:nt, :], in0=pY[0:nt, :], in1=xr[0:nt, :], op=ALU.add),
                       reads=[r_pY, rxr], writes=[ryo])
                  S.dma("sp", lambda e: e.dma_start(out=ydst[th * 128:th * 128 + nt, ch * 512:(ch + 1) * 512], in_=yo[0:nt, :]),
                        ryo, reads=[ryo])

              def out_proj_deferred(nq, bufs, xsrc, ydst, first):
                  k = 0
                  for th in range((nq + 127) // 128):
                      for ch in range(2):
                          defer.add(first + 3 * k, lambda th=th, ch=ch: out_proj_q(nq, bufs, xsrc, ydst, th, ch))
                          k += 1
                  return first + 3 * k

              load_keys(kT_scr, v_scr, [r_kblk[NQT]], 0, 16, [0])
              load_keys(kTs_scr, vs_scr, [r_kTs, r_vs], 128, 512, [1, 2, 3, 4])
              load_keys(kTs_scr, vs_scr, [r_kTs, r_vs], 640, 512, [5, 6, 7, 8])
              load_keys(kTs_scr, vs_scr, [r_kTs, r_vs], 1152, 16, [9])
              bufs_s = load_qtile(NQT, 16, None)
              for _k in range(32):
                  defer.add(_k + 1, lambda _k=_k: wout_chunk(_k // 4, _k % 4))
              attention(16, bufs_s, [(0, 16, None)] + [(t, 128, None) for t in range(1, 9)] + [(9, 16, None)], next_nkt=5)
              out_proj_deferred(16, bufs_s, x_aux[16:32, :], y_s, 18)

              stage(7)
              def prefetch(j):
                  b = load_qtile(j, 256, None)
                  load_keys(kT_scr, v_scr, [r_kblk[j]], 128 + 512 * j, 512, [4 * j + 1 + i for i in range(4)])
                  return b

              pending = {0: prefetch(0)}
              for j in range(NQT):
                  bufs = pending.pop(j)
                  kts = [(0, 16, None)] + [(t, 128, None) for t in range(1, 4 * j + 1)]
                  kts += [(4 * j + 1, 128, "diagA"), (4 * j + 2, 128, "diagB"), (4 * j + 3, 128, "role"), (4 * j + 4, 128, "role")]
                  if j + 1 < NQT:
                      defer.add(31, lambda j=j: pending.__setitem__(j + 1, prefetch(j + 1)))
                  attention(256, bufs, kts)
                  out_proj_deferred(256, bufs, x_own[j * 256:(j + 1) * 256, :], y_own[j * 256:(j + 1) * 256, :], 18)
                  if j + 1 < NQT and (j + 1) not in pending:
                      defer.flush()
              defer.flush()
              S.barrier()
        except _Stop:
            pass
        S.emit()
    return nc


_NC_CACHE = {}


def _rope_tables(pos):
    inv = (500000.0 ** (-np.arange(0, 16, 2, dtype=np.float32) / 16)).astype(np.float32)
    ang = pos.astype(np.float32)[:, None] * inv[None, :]
    return np.concatenate([np.cos(ang), np.cos(ang), -np.sin(ang), np.sin(ang)], axis=1).astype(np.float32)


def make_in_maps(x_prompt, x_sample, cache_k, cache_v, state_conv, meta_tokens, ln_g, w_in, q_norm_g, k_norm_g,
                 lam_q1, lam_k1, lam_q2, lam_k2, subln_g, conv_w, conv_b, w_out, cores=range(8)):
    f = lambda a: np.ascontiguousarray(np.asarray(a, dtype=np.float32))
    x_prompt = f(x_prompt); x_sample = f(x_sample); cache_k = f(cache_k); cache_v = f(cache_v)
    state_conv = f(state_conv); meta_tokens = f(meta_tokens)
    ident = np.eye(128, dtype=np.float32).astype(ml_dtypes.bfloat16)
    qk_g = np.stack([f(q_norm_g)[0], f(k_norm_g)[0]])
    lam_in = np.stack([f(lam_q1)[0], f(lam_k1)[0], f(lam_q2)[0], f(lam_k2)[0]])
    in_maps = []
    tok_idx = {}
    for c in cores:
        b, r = c // 2, c % 2
        own = (np.arange(NQT)[:, None] * 512 + 256 * r + np.arange(256)[None, :]).reshape(-1)
        oth = (np.arange(NQT)[:, None] * 512 + 256 * (1 - r) + np.arange(256)[None, :]).reshape(-1)
        tok_idx[c] = own
        x_aux = np.zeros((128, 1024), np.float32)
        x_aux[0:16] = meta_tokens
        x_aux[16:32] = x_sample[c]
        pos_aux = np.zeros(128, np.float32)
        pos_aux[0:16] = np.arange(16)
        pos_aux[16:32] = 16 + 1024 + np.arange(16)
        for j in range(NQT):
            t0 = 512 * j + 256 * r
            for d in range(2):
                t = t0 - 2 + d
                x_aux[32 + 2 * j + d] = meta_tokens[16 + t] if t < 0 else x_prompt[b, t]
        in_maps.append(dict(
            x_own=np.ascontiguousarray(x_prompt[b, own]), x_oth=np.ascontiguousarray(x_prompt[b, oth]), x_aux=x_aux,
            cs_own=_rope_tables(16 + own), cs_oth=_rope_tables(16 + oth), cs_aux=_rope_tables(pos_aux),
            cache_k=np.ascontiguousarray(cache_k[0, c].reshape(1024, 512)),
            cache_v=np.ascontiguousarray(cache_v[0, c].reshape(1024, 512)),
            state_conv=np.ascontiguousarray(state_conv[0, c]),
            w_in=f(w_in)[0], w_out=f(w_out)[0], ln_g=f(ln_g)[0], qk_g=qk_g, lam_in=lam_in, subln_g=f(subln_g)[0],
            conv_w=f(conv_w)[0], conv_b=f(conv_b)[0], ident=ident,
            role_bias=np.full((128, 1), 0.0 if r == 1 else NEG, np.float32),
        ))
    return in_maps, tok_idx


def kernel(**inputs):
    B = 4
    if "nc" not in _NC_CACHE:
        _NC_CACHE["nc"] = build_program(16)
    nc = _NC_CACHE["nc"]
    in_maps, tok_idx = make_in_maps(**inputs)
    res = run_bass_kernel_spmd(nc, in_maps, core_ids=list(range(8)))
    R = res.results

    y_prompt = np.zeros((B, S_TOK, 1024), np.float32)
    k_prompt = np.zeros((1, B, 16 + S_TOK, 4, 128), np.float32)
    v_prompt = np.zeros((1, B, 16 + S_TOK, 4, 128), np.float32)
    conv_prompt = np.zeros((1, B, 2, 512), np.float32)
    y_sample = np.zeros((8, 16, 1024), np.float32)
    k_sample = np.zeros((1, 8, 16, 4, 128), np.float32)
    v_sample = np.zeros((1, 8, 16, 4, 128), np.float32)
    conv_sample = np.zeros((1, 8, 2, 512), np.float32)
    for c in range(8):
        b, r = c // 2, c % 2
        own = tok_idx[c]
        y_prompt[b, own] = R[c]["y_own"]
        k_prompt[0, b, 16 + own] = R[c]["k_own"].reshape(N_OWN, 4, 128)
        v_prompt[0, b, 16 + own] = R[c]["v_own"].reshape(N_OWN, 4, 128)
        if r == 0:
            k_prompt[0, b, 0:16] = R[c]["k_aux"][0:16].reshape(16, 4, 128)
            v_prompt[0, b, 0:16] = R[c]["v_aux"][0:16].reshape(16, 4, 128)
        else:
            conv_prompt[0, b] = R[c]["conv_p"]
        y_sample[c] = R[c]["y_s"]
        k_sample[0, c] = R[c]["k_aux"][16:32].reshape(16, 4, 128)
        v_sample[0, c] = R[c]["v_aux"][16:32].reshape(16, 4, 128)
        conv_sample[0, c] = R[c]["conv_s"]
    return (y_prompt, y_sample, k_prompt, v_prompt, conv_prompt, k_sample, v_sample, conv_sample)
```
